# Optimizing a Trainium2 kernel written in Bass

```python
import jax, jax.numpy as jnp
from jax import lax
import numpy as np

D_MODEL = 1024
BATCH = 8
SEQ = 8192
DEPTH = 2
DEC_BATCH = 8
DEC_SEQ = 2048
PAST_LEN = 128

RET_HEADS = 6
RET_HEAD_DIM = 64
RET_W = RET_HEADS * RET_HEAD_DIM
RET_CHUNK = 128
MLA_HEADS = 6
MLA_NOPE = 64
MLA_ROPE = 32
MLA_V = 64
MLA_Q_LORA = 384
MLA_KV_LORA = 128
MLA_W = MLA_HEADS * MLA_V
Q_BLOCK = 128
CONV_CH = D_MODEL - RET_W - MLA_W
CONV_K = 31
D_FF = 2816
FFN_CONV_K = 3
ROPE_BASE = 10000.0
EPS = 1e-6
IN_COLS = 4 * RET_W + MLA_Q_LORA + MLA_KV_LORA + MLA_ROPE + 2 * CONV_CH

kernel_name = "hymba_style_retention_mla_conformer_encoder"


def rmsnorm(x, w):
    xf = x.astype(jnp.float32)
    y = xf * lax.rsqrt(jnp.mean(xf * xf, axis=-1, keepdims=True) + EPS)
    return (y * w.astype(jnp.float32)).astype(x.dtype)


def layernorm(x, w, b):
    xf = x.astype(jnp.float32)
    mu = jnp.mean(xf, axis=-1, keepdims=True)
    var = jnp.mean(jnp.square(xf - mu), axis=-1, keepdims=True)
    y = (xf - mu) * lax.rsqrt(var + EPS)
    return (y * w.astype(jnp.float32) + b.astype(jnp.float32)).astype(x.dtype)


def rope(x, pos):
    d = x.shape[-1]
    half = d // 2
    freqs = 1.0 / (ROPE_BASE ** (jnp.arange(half, dtype=jnp.float32) / half))
    ang = pos[:, None] * freqs[None, :]
    cos = jnp.cos(ang)[:, None, :].astype(x.dtype)
    sin = jnp.sin(ang)[:, None, :].astype(x.dtype)
    x1, x2 = x[..., :half], x[..., half:]
    return jnp.concatenate([x1 * cos - x2 * sin, x1 * sin + x2 * cos], axis=-1)


def dwconv(x, w, b):
    out = lax.conv_general_dilated(x, w[:, None, :].astype(x.dtype), window_strides=(1,), padding="SAME",
                                   dimension_numbers=("NWC", "WIO", "NWC"),
                                   feature_group_count=x.shape[-1])
    return out + b.astype(x.dtype)


def decay_log(offset):
    return jnp.log(1.0 - 2.0 ** (-offset - jnp.arange(RET_HEADS, dtype=jnp.float32)))


def retention_scan(q, k, v, log_gamma, include_diag):
    B, S, H, dk = q.shape
    dv = v.shape[-1]
    C = RET_CHUNK
    N = S // C
    qc = q.reshape(B, N, C, H, dk).transpose(1, 0, 3, 2, 4)
    kc = k.reshape(B, N, C, H, dk).transpose(1, 0, 3, 2, 4)
    vc = v.reshape(B, N, C, H, dv).transpose(1, 0, 3, 2, 4)
    idx = jnp.arange(C, dtype=jnp.float32)
    diff = idx[:, None] - idx[None, :]
    mask = (diff >= 0) if include_diag else (diff > 0)
    decay_intra = jnp.where(mask[None], jnp.exp(log_gamma[:, None, None] * jnp.where(mask, diff, 0.0)[None]), 0.0)
    q_decay = jnp.exp(log_gamma[:, None] * (idx[None, :] + 1.0))[None, :, :, None]
    k_decay = jnp.exp(log_gamma[:, None] * (C - 1.0 - idx[None, :]))[None, :, :, None]
    chunk_decay = jnp.exp(log_gamma * C)[None, :, None, None]

    def step(state, inp):
        qb, kb, vb = inp
        scores = jnp.einsum("bhid,bhjd->bhij", qb, kb) * decay_intra[None]
        intra = jnp.einsum("bhij,bhjv->bhiv", scores, vb)
        cross = jnp.einsum("bhid,bhdv->bhiv", qb, state) * q_decay
        new_state = state * chunk_decay + jnp.einsum("bhjd,bhjv->bhdv", kb * k_decay, vb)
        return new_state, intra + cross

    state0 = jnp.zeros((B, H, dk, dv), jnp.float32)
    _, out = lax.scan(step, state0, (qc, kc, vc))
    return out.transpose(1, 0, 3, 2, 4).reshape(B, S, H, dv)


def retention_bidir(q, k, v):
    fwd = retention_scan(q, k, v, decay_log(5.0), True)
    bwd = retention_scan(q[:, ::-1], k[:, ::-1], v[:, ::-1], decay_log(5.5), False)[:, ::-1]
    return fwd + bwd


def mla_attend(qn, qr, kn, kr, v):
    B, S, H, dn = qn.shape
    dr = qr.shape[-1]
    dv = v.shape[-1]
    NB = S // Q_BLOCK
    scale = (MLA_NOPE + MLA_ROPE) ** -0.5
    qn_b = qn.reshape(B, NB, Q_BLOCK, H, dn).transpose(1, 0, 2, 3, 4)
    qr_b = qr.reshape(B, NB, Q_BLOCK, H, dr).transpose(1, 0, 2, 3, 4)

    def blk(args):
        qnb, qrb = args
        s = jnp.einsum("bqhd,bkhd->bhqk", qnb, kn) + jnp.einsum("bqhr,bkr->bhqk", qrb, kr)
        p = jax.nn.softmax(s.astype(jnp.float32) * scale, axis=-1)
        return jnp.einsum("bhqk,bkhv->bqhv", p.astype(v.dtype), v)

    o = lax.map(blk, (qn_b, qr_b))
    return o.transpose(1, 0, 2, 3, 4).reshape(B, S, H, dv)


def encoder_layer(x, attn_norm_w, w_in, ret_gn_w, mla_q_norm_w, mla_w_uq, mla_kv_norm_w, mla_w_ukv,
                  conv_dw_w, conv_dw_b, conv_ln_w, conv_ln_b, conv_pw_w, conv_pw_b, w_out,
                  ffn_norm_w, w_up, ffn_conv_w, ffn_conv_b, w_down):
    B, S, _ = x.shape
    pos = jnp.arange(S, dtype=jnp.float32)
    h = rmsnorm(x, attn_norm_w)
    proj = h @ w_in
    cuts = np.cumsum([RET_W, RET_W, RET_W, RET_W, MLA_Q_LORA, MLA_KV_LORA, MLA_ROPE]).tolist()
    q_r, k_r, v_r, g_r, cq, ckv, kr_raw, conv_in = jnp.split(proj, cuts, axis=-1)

    q = rope(q_r.reshape(B, S, RET_HEADS, RET_HEAD_DIM), pos).astype(jnp.float32)
    k = (rope(k_r.reshape(B, S, RET_HEADS, RET_HEAD_DIM), pos) * (RET_HEAD_DIM ** -0.5)).astype(jnp.float32)
    v = v_r.reshape(B, S, RET_HEADS, RET_HEAD_DIM).astype(jnp.float32)
    y = retention_bidir(q, k, v)
    mu = jnp.mean(y, axis=-1, keepdims=True)
    var = jnp.mean(jnp.square(y - mu), axis=-1, keepdims=True)
    y = ((y - mu) * lax.rsqrt(var + EPS)).reshape(B, S, RET_W) * ret_gn_w.astype(jnp.float32)
    y_ret = jax.nn.silu(g_r) * y.astype(x.dtype)

    cq_n = rmsnorm(cq, mla_q_norm_w)
    qm = (cq_n @ mla_w_uq).reshape(B, S, MLA_HEADS, MLA_NOPE + MLA_ROPE)
    qn = qm[..., :MLA_NOPE]
    qr = rope(qm[..., MLA_NOPE:], pos)
    ckv_n = rmsnorm(ckv, mla_kv_norm_w)
    kv = (ckv_n @ mla_w_ukv).reshape(B, S, MLA_HEADS, MLA_NOPE + MLA_V)
    kn = kv[..., :MLA_NOPE]
    vm = kv[..., MLA_NOPE:]
    kr = rope(kr_raw[:, :, None, :], pos)[:, :, 0, :]
    y_mla = mla_attend(qn, qr, kn, kr, vm).reshape(B, S, MLA_W)

    a, gt = jnp.split(conv_in, 2, axis=-1)
    c = a * jax.nn.sigmoid(gt)
    c = dwconv(c, conv_dw_w, conv_dw_b)
    c = jax.nn.silu(layernorm(c, conv_ln_w, conv_ln_b))
    y_conv = c @ conv_pw_w + conv_pw_b

    x = x + jnp.concatenate([y_ret, y_mla, y_conv], axis=-1) @ w_out

    h2 = rmsnorm(x, ffn_norm_w)
    u = dwconv(h2 @ w_up, ffn_conv_w, ffn_conv_b)
    gate, up = jnp.split(u, 2, axis=-1)
    return x + (jax.nn.silu(gate) * up) @ w_down


def setup_inputs(seed: int = 0) -> dict:
    key = jax.random.key(seed)
    ks = jax.random.split(key, 24)
    f32 = jnp.float32

    def nrm(k, shape, scale):
        return jax.random.normal(k, shape, f32) * scale

    def gain(k, shape):
        return 1.0 + 0.01 * jax.random.normal(k, shape, f32)

    L = DEPTH
    return {
        "x_prompt": jax.random.normal(ks[0], (BATCH, SEQ, D_MODEL), f32),
        "x_sample": jax.random.normal(ks[1], (DEC_BATCH, DEC_SEQ, D_MODEL), f32),
        "attn_norm_w": gain(ks[2], (L, D_MODEL)),
        "w_in": nrm(ks[3], (L, D_MODEL, IN_COLS), D_MODEL ** -0.5),
        "ret_gn_w": gain(ks[4], (L, RET_W)),
        "mla_q_norm_w": gain(ks[5], (L, MLA_Q_LORA)),
        "mla_w_uq": nrm(ks[6], (L, MLA_Q_LORA, MLA_HEADS * (MLA_NOPE + MLA_ROPE)), MLA_Q_LORA ** -0.5),
        "mla_kv_norm_w": gain(ks[7], (L, MLA_KV_LORA)),
        "mla_w_ukv": nrm(ks[8], (L, MLA_KV_LORA, MLA_HEADS * (MLA_NOPE + MLA_V)), MLA_KV_LORA ** -0.5),
        "conv_dw_w": nrm(ks[9], (L, CONV_K, CONV_CH), CONV_K ** -0.5),
        "conv_dw_b": nrm(ks[10], (L, CONV_CH), 0.01),
        "conv_ln_w": gain(ks[11], (L, CONV_CH)),
        "conv_ln_b": nrm(ks[12], (L, CONV_CH), 0.01),
        "conv_pw_w": nrm(ks[13], (L, CONV_CH, CONV_CH), CONV_CH ** -0.5),
        "conv_pw_b": nrm(ks[14], (L, CONV_CH), 0.01),
        "w_out": nrm(ks[15], (L, D_MODEL, D_MODEL), D_MODEL ** -0.5),
        "ffn_norm_w": gain(ks[16], (L, D_MODEL)),
        "w_up": nrm(ks[17], (L, D_MODEL, 2 * D_FF), D_MODEL ** -0.5),
        "ffn_conv_w": nrm(ks[18], (L, FFN_CONV_K, 2 * D_FF), FFN_CONV_K ** -0.5),
        "ffn_conv_b": nrm(ks[19], (L, 2 * D_FF), 0.01),
        "w_down": nrm(ks[20], (L, D_FF, D_MODEL), D_FF ** -0.5),
        "final_norm_w": gain(ks[21], (D_MODEL,)),
    }


def reference(x_prompt, x_sample, attn_norm_w, w_in, ret_gn_w, mla_q_norm_w, mla_w_uq, mla_kv_norm_w,
              mla_w_ukv, conv_dw_w, conv_dw_b, conv_ln_w, conv_ln_b, conv_pw_w, conv_pw_b, w_out,
              ffn_norm_w, w_up, ffn_conv_w, ffn_conv_b, w_down, final_norm_w):
    xp = x_prompt
    xs = x_sample
    for l in range(DEPTH):
        lw = (attn_norm_w[l], w_in[l], ret_gn_w[l], mla_q_norm_w[l], mla_w_uq[l], mla_kv_norm_w[l],
              mla_w_ukv[l], conv_dw_w[l], conv_dw_b[l], conv_ln_w[l], conv_ln_b[l], conv_pw_w[l],
              conv_pw_b[l], w_out[l], ffn_norm_w[l], w_up[l], ffn_conv_w[l], ffn_conv_b[l], w_down[l])
        xp = encoder_layer(xp, *lw)
        xs = encoder_layer(xs, *lw)
    y_prompt = rmsnorm(xp, final_norm_w)
    y_sample = rmsnorm(xs, final_norm_w)
    return (y_prompt, y_sample)
```

```python
import numpy as np
import ml_dtypes
import concourse.bass as bass
import concourse.mybir as mybir
from concourse.bass_utils import run_bass_kernel_spmd

F32 = mybir.dt.float32
BF16 = mybir.dt.bfloat16
AF = mybir.ActivationFunctionType
ALU = mybir.AluOpType
AX = mybir.AxisListType

P = 128
D = 1024
H = 6
RW = 384
INC = 2592
DFF = 2816
NJ = 22
EPS = 1e-6
SC = float(96 ** -0.5)
C_AN, C_QN, C_KVN, C_FN, C_FCW, C_FCB, C_PWB, C_DW, NCOLS = 0, 8, 11, 12, 20, 152, 196, 198, 260
B_GN, B_DWB, B_LNW, B_LNB, NBROW = 0, 384, 640, 896, 1152


class Buf:
    __slots__ = ("w", "r")

    def __init__(self):
        self.w = {}
        self.r = {}


class Tile:
    def __init__(self, ap):
        self.ap = ap
        self.buf = Buf()
        self.ds = None


class DSem:
    def __init__(self, name, sem):
        self.name = name
        self.sem = sem
        self.val = 0


class Sched:
    ENG = ("pe", "act", "dve", "pool", "sp")

    def __init__(self, nc):
        self.nc = nc
        self.ops = {n: [] for n in self.ENG}
        self.esem = {n: nc.alloc_semaphore(name="es_" + n) for n in ("pe", "act", "dve", "pool")}
        self.ecnt = {n: 0 for n in self.esem}
        self.seen = {n: {} for n in self.ENG}
        self.dpool = []
        self.dnext = 0

    def new_phase(self):
        self.dnext = 0

    def dsem(self):
        if self.dnext >= len(self.dpool):
            nm = "ds%d" % len(self.dpool)
            self.dpool.append(DSem(nm, self.nc.alloc_semaphore(name=nm)))
        d = self.dpool[self.dnext]
        self.dnext += 1
        return d

    def op(self, eng, fn, reads=(), writes=(), partial=(), dma=None):
        if eng == "pool" and dma is None and not OPT.get('usepool'):
            eng = "dve"
        if eng == "pool!":
            eng = "pool"
        need = {}

        def add(d):
            for k, sv in d.items():
                if k not in need or need[k][1] < sv[1]:
                    need[k] = sv

        for t in reads:
            add(t.buf.w)
            if getattr(t, 'psum', False) and eng != 'pe':
                for k, sv in t.buf.r.items():
                    if k != 'es_' + eng and (k not in need or need[k][1] < sv[1]):
                        need[k] = sv
        for t in writes:
            add(t.buf.w)
            add(t.buf.r)
        for t in partial:
            add(t.buf.r)
        waits = []
        seen = self.seen[eng]
        for k, (s, v) in need.items():
            if eng == "pe" and k == "es_pe":
                continue
            if seen.get(k, 0) >= v:
                continue
            seen[k] = v
            waits.append((s, v))
        if dma is None:
            sem = self.esem[eng]
            self.ecnt[eng] += 1
            val = self.ecnt[eng]
            inc = 1
            key = "es_" + eng
        else:
            dma.val += 16
            sem, val, inc, key = dma.sem, dma.val, 16, dma.name
        for t in writes:
            t.buf.w = {key: (sem, val)}
            t.buf.r = {}
        for t in partial:
            t.buf.w[key] = (sem, val)
        for t in reads:
            t.buf.r[key] = (sem, val)
        self.ops[eng].append((fn, waits, sem, inc))

    def barrier(self):
        cur = {}
        for n, s in self.esem.items():
            cur["es_" + n] = (s, self.ecnt[n])
        for d in self.dpool:
            cur[d.name] = (d.sem, d.val)
        for eng in self.ENG:
            waits = []
            for k, (s, v) in cur.items():
                if v > 0 and self.seen[eng].get(k, 0) < v:
                    self.seen[eng][k] = v
                    waits.append((s, v))
            if waits:
                self.ops[eng].append((None, waits, None, 0))

    def emit(self):
        nc = self.nc
        with nc.Block() as block:
            for name, deco in (("sp", block.sync), ("act", block.scalar), ("pe", block.tensor),
                               ("dve", block.vector), ("pool", block.gpsimd)):
                def body(e, name=name):
                    for fn, waits, sem, inc in self.ops[name]:
                        for s, v in waits:
                            e.wait_ge(s, v)
                        if fn is not None:
                            fn(e).then_inc(sem, inc)
                deco(body)


class Arena:
    def __init__(self, nc, name, nelem, dt):
        self.t = nc.alloc_sbuf_tensor(name, [P, nelem], dt)
        self.cap = nelem
        self.off = 0
        self.al = 16 if dt == F32 else 32

    def reset(self):
        self.off = 0

    def get(self, *shape):
        n = int(np.prod(shape))
        n_al = (n + self.al - 1) // self.al * self.al
        assert self.off + n_al <= self.cap, ("arena overflow", self.off, n_al, self.cap)
        ap = self.t[:, self.off:self.off + n]
        self.off += n_al
        if len(shape) == 2:
            ap = ap.rearrange("p (a b) -> p a b", b=shape[1])
        elif len(shape) == 3:
            ap = ap.rearrange("p (a b c) -> p a b c", b=shape[1], c=shape[2])
        return Tile(ap)


def bc_mid(ap, reps):
    a = [list(x) for x in ap.ap]
    return bass.AP(ap.tensor, ap.offset, [a[0], [0, reps]] + a[1:])


def bc_last(ap, reps):
    a = [list(x) for x in ap.ap]
    return bass.AP(ap.tensor, ap.offset, a + [[0, reps]])


def host_consts(smax):
    C = 128
    pos = np.arange(smax, dtype=np.float32)

    def tab(half):
        fr = (1.0 / (np.float32(10000.0) ** (np.arange(half, dtype=np.float32) / np.float32(half)))).astype(np.float32)
        ang = (pos[:, None] * fr[None, :]).astype(np.float32)
        c = np.cos(ang).astype(np.float32)
        s = np.sin(ang).astype(np.float32)
        return np.concatenate([c, c, -s, s], axis=1)

    rope = np.concatenate([tab(32), tab(16)], axis=1).astype(np.float32)
    h = np.arange(H, dtype=np.float64)
    gf = 1.0 - 2.0 ** (-5.0 - h)
    gb = 1.0 - 2.0 ** (-5.5 - h)
    idx = np.arange(C, dtype=np.float64)
    dif = idx[None, :] - idx[:, None]
    mk = np.zeros((C, H, C))
    for hh in range(H):
        mk[:, hh, :] = np.where(dif >= 0, gf[hh] ** np.maximum(dif, 0), gb[hh] ** np.maximum(-dif, 0))
    qdf = np.zeros((C, 3, C))
    qdb = np.zeros((C, 3, C))
    cdf = np.zeros((C, 3, 64))
    cdb = np.zeros((C, 3, 64))
    for hh in range(H):
        b, pp = hh // 2, hh % 2
        qdf[pp * 64:(pp + 1) * 64, b, :] = (gf[hh] ** (idx + 1.0))[None, :]
        qdb[pp * 64:(pp + 1) * 64, b, :] = (gb[hh] ** (C - idx))[None, :]
        cdf[pp * 64:(pp + 1) * 64, b, :] = gf[hh] ** C
        cdb[pp * 64:(pp + 1) * 64, b, :] = gb[hh] ** C
    kdf = np.zeros((C, H, 64))
    kdb = np.zeros((C, H, 64))
    for hh in range(H):
        kdf[:, hh, :] = (gf[hh] ** (C - 1.0 - idx))[:, None]
        kdb[:, hh, :] = (gb[hh] ** idx)[:, None]
    sel = np.zeros((8, 128), np.float32)
    sel[:, 96] = 1.0
    k64 = np.zeros((128, 6), np.float32)
    k64[64:96, :] = 1.0
    selb = np.zeros((128, 64), np.float32)
    selb[64, :] = 1.0
    return {
        "c_selb": selb,
        "c_rope": rope,
        "c_mask": mk.reshape(C, H * C).astype(np.float32),
        "c_qdf": qdf.reshape(C, 384).astype(np.float32),
        "c_qdb": qdb.reshape(C, 384).astype(np.float32),
        "c_cdf": cdf.reshape(C, 192).astype(np.float32),
        "c_cdb": cdb.reshape(C, 192).astype(np.float32),
        "c_kdf": kdf.reshape(C, 384).astype(np.float32),
        "c_kdb": kdb.reshape(C, 384).astype(np.float32),
        "c_idb": np.eye(128).astype(ml_dtypes.bfloat16),
        "c_idf": np.eye(128).astype(np.float32),
        "c_sel": sel,
        "c_k64": k64,
    }


WNAMES = ["attn_norm_w", "w_in", "ret_gn_w", "mla_q_norm_w", "mla_w_uq", "mla_kv_norm_w", "mla_w_ukv",
          "conv_dw_w", "conv_dw_b", "conv_ln_w", "conv_ln_b", "conv_pw_w", "conv_pw_b", "w_out",
          "ffn_norm_w", "w_up", "ffn_conv_w", "ffn_conv_b", "w_down", "final_norm_w"]
WSHAPES = {
    "attn_norm_w": [D], "w_in": [D, INC], "ret_gn_w": [RW], "mla_q_norm_w": [384], "mla_w_uq": [384, 576],
    "mla_kv_norm_w": [128], "mla_w_ukv": [128, 768], "conv_dw_w": [31, 256], "conv_dw_b": [256],
    "conv_ln_w": [256], "conv_ln_b": [256], "conv_pw_w": [256, 256], "conv_pw_b": [256], "w_out": [D, D],
    "ffn_norm_w": [D], "w_up": [D, 2 * DFF], "ffn_conv_w": [3, 2 * DFF], "ffn_conv_b": [2 * DFF], "w_down": [DFF, D],
}


def build(seqs, depth, smax, dbg=None, stop_after=None):
    nc = bass.Bass("TRN2", target_bir_lowering=False)
    S_ = Sched(nc)
    dbg = dbg or []

    def dram(name, shape, dt, kind="Internal"):
        if name in dbg:
            kind = "ExternalOutput"
        return nc.dram_tensor(name, list(shape), dt, kind=kind).ap()

    W = {}
    for n in WNAMES:
        shp = ([depth] + WSHAPES[n]) if n != "final_norm_w" else [D]
        W[n] = nc.dram_tensor(n, shp, F32, kind="ExternalInput").ap()
    hc = host_consts(smax)
    CIN = {k: nc.dram_tensor(k, list(v.shape), BF16 if v.dtype == ml_dtypes.bfloat16 else F32,
                             kind="ExternalInput").ap() for k, v in hc.items()}
    XIN, YOUT, SC_ = {}, {}, {}
    for (sn, S) in seqs:
        XIN[sn] = nc.dram_tensor("x_" + sn, [S, D], F32, kind="ExternalInput").ap()
        YOUT[sn] = nc.dram_tensor("y_" + sn, [S, D], F32, kind="ExternalOutput").ap()
        NB = S // P
        SC_[sn] = dict(
            QTR=dram("QTR_" + sn, [3, P, S], BF16), KTR=dram("KTR_" + sn, [3, P, S], BF16),
            KR=dram("KR_" + sn, [S, RW], BF16), VR=dram("VR_" + sn, [S, RW], BF16), G=dram("G_" + sn, [S, RW], F32),
            QTM=dram("QTM_" + sn, [H, P, S], BF16), KTM=dram("KTM_" + sn, [H, P, S], BF16),
            VM=dram("VM_" + sn, [H, P, NB, 65], BF16), CT=dram("CT_" + sn, [2, P, S + 30], BF16),
            YT=dram("YT_" + sn, [8, P, S], BF16), X1=dram("X1_" + sn, [S, D], F32), X2=dram("X2_" + sn, [S, D], F32),
        )
    WS = []
    for l in range(depth):
        WS.append(dict(
            WIN=dram("WIN%d" % l, [P, 8, INC], BF16), WUQ=dram("WUQ%d" % l, [P, 3, 576], BF16),
            WUKV=dram("WUKV%d" % l, [P, 768], BF16), WPW=dram("WPW%d" % l, [P, 2, 256], BF16),
            WOUT=dram("WOUT%d" % l, [P, 8, D], BF16), WUP=dram("WUP%d" % l, [NJ, P, 8, 256], BF16),
            WDN=dram("WDN%d" % l, [P, NJ, D], BF16)))
    ZROW = dram("ZROW", [1, D], F32)

    def sb(name, shape, dt):
        t = nc.alloc_sbuf_tensor(name, [P] + list(shape), dt)
        idx = (slice(None),) * (len(shape) + 1)
        return Tile(t[idx])

    IDB = sb("idb", [128], BF16)
    IDF = sb("idf", [128], F32)
    COLS = sb("cols", [depth, NCOLS], F32)
    BROWS = sb("brows", [depth, NBROW], F32)
    FNW = sb("fnw", [D], F32)
    MASK = sb("mask", [H, 128], F32)
    QDF = sb("qdf", [3, 128], F32)
    QDB = sb("qdb", [3, 128], F32)
    CDF = sb("cdf", [3, 64], F32)
    CDB = sb("cdb", [3, 64], F32)
    KDF = sb("kdf", [RW], F32)
    KDB = sb("kdb", [RW], F32)
    SEL = sb("sel", [128], F32)
    K64 = sb("k64", [6], F32)
    SELB = sb("selb", [64], F32)
    ZT = sb("zt", [D], F32)
    ONEF = sb("onef", [64], F32)
    EPST = sb("epst", [1], F32)
    AF_ = Arena(nc, "arena_f", 11520, F32)
    AB_ = Arena(nc, "arena_b", 50176, BF16)
    PB = []
    pbig = nc.alloc_psum_tensor("pbig", [P, 4096], F32)
    for i in range(8):
        PB.append(Tile(pbig[:, i * 512:(i + 1) * 512]))
        PB[-1].psum = True
    PB7B = PB[7].ap.bitcast(BF16)

    def dma(eng, out_t, out_ap, in_t, in_ap, partial=False):
        sbt = out_t if out_t is not None else in_t
        if sbt.ds is None:
            sbt.ds = S_.dsem()
        if out_t is not None:
            kw = dict(partial=[out_t]) if partial else dict(writes=[out_t])
        else:
            kw = dict(reads=[in_t])
        S_.op(eng, lambda e: e.dma_start(out=out_ap, in_=in_ap), dma=sbt.ds, **kw)

    def load(out_t, in_ap, out_ap=None, partial=False):
        dma("sp", out_t, out_t.ap if out_ap is None else out_ap, None, in_ap, partial=partial)

    def store(out_ap, in_t, in_ap=None):
        dma("pool", None, out_ap, in_t, in_t.ap if in_ap is None else in_ap)

    def mm(out_t, out_ap, lt, l_ap, rt, r_ap, start, stop):
        S_.op("pe", lambda e: e.matmul(out_ap, l_ap, r_ap, start=start, stop=stop),
              reads=[lt, rt], writes=[out_t])

    def tr(out_t, out_ap, in_t, in_ap, ident_ap, idt):
        S_.op("pe", lambda e: e.transpose(out_ap, in_ap, ident_ap), reads=[in_t, idt], writes=[out_t])

    def act(out_t, out_ap, in_t, in_ap, func, bias=None, scale=None, accum=None, extra=(), partial=False):
        kw = {}
        rd = [in_t] + list(extra)
        wr = [out_t]
        if bias is not None:
            kw["bias"] = bias
        if scale is not None:
            kw["scale"] = scale
        if accum is not None:
            kw["accum_out"] = accum[1]
            wr.append(accum[0])
        if partial:
            S_.op("act", lambda e: e.activation(out=out_ap, in_=in_ap, func=func, **kw), reads=rd, partial=wr)
        else:
            S_.op("act", lambda e: e.activation(out=out_ap, in_=in_ap, func=func, **kw), reads=rd, writes=wr)

    def ts(eng, out_t, out_ap, in_t, in_ap, s1, s2, op0, op1=None, extra=(), partial=False):
        kw = dict(partial=[out_t]) if partial else dict(writes=[out_t])
        if op1 is None:
            S_.op(eng, lambda e: e.tensor_scalar(out=out_ap, in0=in_ap, scalar1=s1, scalar2=None, op0=op0),
                  reads=[in_t] + list(extra), **kw)
        else:
            S_.op(eng, lambda e: e.tensor_scalar(out=out_ap, in0=in_ap, scalar1=s1, scalar2=s2, op0=op0, op1=op1),
                  reads=[in_t] + list(extra), **kw)

    def tt(eng, out_t, out_ap, a_t, a_ap, b_t, b_ap, op, partial=False):
        kw = dict(partial=[out_t]) if partial else dict(writes=[out_t])
        S_.op(eng, lambda e: e.tensor_tensor(out=out_ap, in0=a_ap, in1=b_ap, op=op), reads=[a_t, b_t], **kw)

    def stt(eng, out_t, out_ap, a_t, a_ap, scal, b_t, b_ap, op0, op1, extra=(), partial=False):
        kw = dict(partial=[out_t]) if partial else dict(writes=[out_t])
        S_.op(eng, lambda e: e.scalar_tensor_tensor(out=out_ap, in0=a_ap, scalar=scal, in1=b_ap, op0=op0, op1=op1),
              reads=[a_t, b_t] + list(extra), **kw)

    def cp(eng, out_t, out_ap, in_t, in_ap, partial=False):
        kw = dict(partial=[out_t]) if partial else dict(writes=[out_t])
        if eng == "act":
            S_.op("act", lambda e: e.activation(out=out_ap, in_=in_ap, func=AF.Copy), reads=[in_t], **kw)
        else:
            S_.op(eng, lambda e: e.tensor_copy(out=out_ap, in_=in_ap), reads=[in_t], **kw)

    def red(out_t, out_ap, in_t, in_ap, partial=False):
        kw = dict(partial=[out_t]) if partial else dict(writes=[out_t])
        S_.op("dve", lambda e: e.tensor_reduce(out=out_ap, in_=in_ap, axis=AX.X, op=ALU.add), reads=[in_t], **kw)

    def mset(eng, t, ap, val, partial=False):
        kw = dict(partial=[t]) if partial else dict(writes=[t])
        S_.op(eng, lambda e: e.memset(ap, val), **kw)

    def rstd_from_ss(ss_t, out_t, n):
        rsq(out_t, out_t.ap, ss_t, ss_t.ap, 1.0 / n)

    def recip(t, ap, partial=False):
        kw = dict(partial=[t]) if partial else dict(writes=[t])
        S_.op("dve", lambda e: e.reciprocal(out=ap, in_=ap), reads=[t], **kw)

    def rsq(out_t, out_ap, in_t, in_ap, mult, partial=False):
        np_ = out_ap.shape[0]
        act(out_t, out_ap, in_t, in_ap, AF.Sqrt, bias=EPST.ap[0:np_, :], scale=mult, extra=[EPST], partial=partial)
        recip(out_t, out_ap, partial=partial)

    for t, k in ((IDB, "c_idb"), (IDF, "c_idf"), (MASK, "c_mask"), (QDF, "c_qdf"), (QDB, "c_qdb"), (CDF, "c_cdf"),
                 (CDB, "c_cdb"), (KDF, "c_kdf"), (KDB, "c_kdb"), (K64, "c_k64"), (SELB, "c_selb")):
        flat = t.ap if len(t.ap.shape) == 2 else t.ap.rearrange("p a b -> p (a b)")
        load(t, CIN[k][:, :], out_ap=flat)
    load(SEL, CIN["c_sel"][:, :], out_ap=SEL.ap[0:8, :])
    load(FNW, W["final_norm_w"].partition_broadcast(P))
    mset("dve", ZT, ZT.ap, 0.0)
    mset("dve", ONEF, ONEF.ap, 1.0)
    mset("dve", EPST, EPST.ap, EPS)
    store(ZROW[:, :], ZT, ZT.ap[0:1, :])
    for l in range(depth):
        for off, nm, n in ((B_GN, "ret_gn_w", 384), (B_DWB, "conv_dw_b", 256), (B_LNW, "conv_ln_w", 256),
                           (B_LNB, "conv_ln_b", 256)):
            load(BROWS, W[nm][l].partition_broadcast(P), out_ap=BROWS.ap[:, l, off:off + n], partial=True)

    AF_.reset()
    ctmp = [AF_.get(128) for _ in range(2)]
    cnt = [0]

    def load_cols(src2d, n, l, off):
        t = ctmp[cnt[0] % 2]
        cnt[0] += 1
        load(t, src2d, out_ap=t.ap[0:n, :])
        tr(PB[0], PB[0].ap[:, 0:n], t, t.ap[0:n, :], IDF.ap[0:n, 0:n], IDF)
        cp("dve", COLS, COLS.ap[:, l, off:off + n], PB[0], PB[0].ap[:, 0:n], partial=True)

    for l in range(depth):
        load_cols(W["attn_norm_w"][l].rearrange("(c p) -> c p", p=P), 8, l, C_AN)
        load_cols(W["mla_q_norm_w"][l].rearrange("(c p) -> c p", p=P), 3, l, C_QN)
        load_cols(W["mla_kv_norm_w"][l].rearrange("(c p) -> c p", p=P), 1, l, C_KVN)
        load_cols(W["ffn_norm_w"][l].rearrange("(c p) -> c p", p=P), 8, l, C_FN)
        for k in range(3):
            load_cols(W["ffn_conv_w"][l][k].rearrange("(c p) -> c p", p=P), 44, l, C_FCW + 44 * k)
        load_cols(W["ffn_conv_b"][l].rearrange("(c p) -> c p", p=P), 44, l, C_FCB)
        load_cols(W["conv_pw_b"][l].rearrange("(c p) -> c p", p=P), 2, l, C_PWB)
        for c in range(2):
            load_cols(W["conv_dw_w"][l][:, c * 128:(c + 1) * 128], 31, l, C_DW + 31 * c)
    S_.barrier()

    S_.new_phase()
    AF_.reset()
    AB_.reset()
    pl = [AF_.get(2592) for _ in range(2)]
    po = [AB_.get(2592) for _ in range(2)]
    pc = [0]

    def prep_chunk(src_ap, w, dst_ap, scale_ap, dst_view=None):
        i = pc[0]
        pc[0] += 1
        a, b = pl[i % 2], po[i % 2]
        load(a, src_ap, out_ap=a.ap[:, 0:w])
        oap = b.ap[:, 0:w]
        if scale_ap is None:
            if i % 2 == 0:
                cp("dve", b, oap, a, a.ap[:, 0:w])
            else:
                cp("act", b, oap, a, a.ap[:, 0:w])
        else:
            if i % 2 == 0:
                ts("dve", b, oap, a, a.ap[:, 0:w], scale_ap, None, ALU.mult, extra=[COLS])
            else:
                act(b, oap, a, a.ap[:, 0:w], AF.Copy, scale=scale_ap, extra=[COLS])
        store(dst_ap, b, oap if dst_view is None else dst_view(oap))

    for l in range(depth):
        ws = WS[l]
        for kc in range(8):
            prep_chunk(W["w_in"][l][kc * P:(kc + 1) * P, :], INC, ws["WIN"][:, kc, :], COLS.ap[:, l, C_AN + kc:C_AN + kc + 1])
        for kc in range(3):
            prep_chunk(W["mla_w_uq"][l][kc * P:(kc + 1) * P, :], 576, ws["WUQ"][:, kc, :], COLS.ap[:, l, C_QN + kc:C_QN + kc + 1])
        prep_chunk(W["mla_w_ukv"][l][:, :], 768, ws["WUKV"][:, :], COLS.ap[:, l, C_KVN:C_KVN + 1])
        for kc in range(2):
            prep_chunk(W["conv_pw_w"][l][kc * P:(kc + 1) * P, :], 256, ws["WPW"][:, kc, :], None)
        for kc in range(8):
            prep_chunk(W["w_out"][l][kc * P:(kc + 1) * P, :], D, ws["WOUT"][:, kc, :], None)
        wupv = ws["WUP"].rearrange("j p k c -> p j k c")
        for kc in range(8):
            for half in range(2):
                for sg in range(2):
                    c0 = half * DFF + sg * 1408
                    prep_chunk(W["w_up"][l][kc * P:(kc + 1) * P, c0:c0 + 1408], 1408,
                               wupv[:, sg * 11:(sg + 1) * 11, kc, half * 128:(half + 1) * 128],
                               COLS.ap[:, l, C_FN + kc:C_FN + kc + 1],
                               dst_view=lambda a: a.rearrange("p (j c) -> p j c", c=128))
        for j in range(NJ):
            prep_chunk(W["w_down"][l][j * P:(j + 1) * P, :], D, ws["WDN"][:, j, :], None)
    S_.barrier()
    if stop_after == "prep":
        S_.emit()
        return nc

    def phase_A(l, sn, S, xsrc):
        sc = SC_[sn]
        ws = WS[l]
        NT = S // P
        S_.new_phase()
        AF_.reset()
        AB_.reset()
        win = AB_.get(8, INC)
        wuq = AB_.get(3, 576)
        wukv = AB_.get(768)
        load(win, ws["WIN"][:, :, :])
        load(wuq, ws["WUQ"][:, :, :])
        load(wukv, ws["WUKV"][:, :])
        zpad = AB_.get(2, 15)
        mset("dve", zpad, zpad.ap, 0.0)
        ctv = sc["CT"].rearrange("c p s -> p c s")
        store(ctv[:, :, 0:15], zpad)
        store(ctv[:, :, S + 15:S + 30], zpad)
        kmaxs = [AF_.get(H) for _ in range(2)]

        def stream(sid):
            base = sid * 1536
            MB = Tile(pbig[:, base:base + 1536])
            MB.psum = True
            TBt = PB[6 + sid]
            TBB = TBt.ap.bitcast(BF16)

            def view(c0, w):
                t = Tile(pbig[:, base + c0:base + c0 + w])
                t.buf = MB.buf
                t.psum = True
                return t

            Pcq, Pckv, Pcv = view(0, 512), view(512, 512), view(1024, 512)
            M0, M1, M2 = view(0, 512), view(512, 512), view(1024, 512)
            Pq, Pk, Pv, Pg = view(0, 384), view(384, 384), view(768, 384), view(1152, 384)
            Uq = [view(0, 512), view(512, 512)]
            Ukv = [view(1024, 512), TBt]
            xt = [AF_.get(D) for _ in range(2)]
            rt = [AF_.get(192) for _ in range(2)]
            ssx = AF_.get(1)
            rsx = AF_.get(1)
            hb = AB_.get(D)
            hT = AB_.get(8, 128)
            tA = AF_.get(RW)
            tB = AF_.get(RW)
            qrot = AB_.get(RW)
            krot = AB_.get(RW)
            vb = AB_.get(RW)
            gs = AF_.get(RW)
            ss2 = AF_.get(2)
            rs2 = AF_.get(2)
            cqn = AB_.get(512)
            junk = AB_.get(D)
            tA2 = AF_.get(32)
            tB2 = AF_.get(32)
            krf = AF_.get(32)
            krss = AF_.get(1)
            sig = AF_.get(256)
            cb = AB_.get(256)
            s1 = AB_.get(8, 128)
            cqT = AB_.get(4, 128)
            qaug = AB_.get(H, 128)
            kaug = AB_.get(H, 128)
            va = AB_.get(H, 65)
            qtm = AB_.get(H, 128)
            ktm = AB_.get(H, 128)
            sq = AF_.get(3, 96)
            ssq = AF_.get(H)
            kss = AF_.get(H)
            kmax = kmaxs[sid]
            tqa = AF_.get(3, 32)
            tqb = AF_.get(3, 32)
            mset("dve", qaug, qaug.ap, 0.0)
            mset("dve", kaug, kaug.ap, 0.0)
            mset("dve", kaug, kaug.ap[:, :, 96:97], 1.0)
            mset("dve", va, va.ap, 1.0)
            mset("dve", kmax, kmax.ap, 0.0)

            def rope_ret(Pt, out_t, rtile, pre):
                P3 = Pt.ap[:, 0:384].rearrange("p (h d) -> p h d", d=64)
                cc = bc_mid(rtile.ap[:, 0:64], H)
                ns = bc_mid(rtile.ap[:, 64:96], H)
                sn_ = bc_mid(rtile.ap[:, 96:128], H)
                A3 = tA.ap.rearrange("p (h d) -> p h d", d=64)
                B3 = tB.ap.rearrange("p (h d) -> p h d", d=64)
                stt("dve", tA, A3, Pt, P3, pre, rtile, cc, ALU.mult, ALU.mult)
                stt("dve", tB, B3[:, :, 0:32], Pt, P3[:, :, 32:64], pre, rtile, ns, ALU.mult, ALU.mult)
                stt("dve", tB, B3[:, :, 32:64], Pt, P3[:, :, 0:32], pre, rtile, sn_, ALU.mult, ALU.mult, partial=True)
                tt("dve", out_t, out_t.ap, tA, tA.ap, tB, tB.ap, ALU.add)

            tiles = list(range(sid, NT, 2))

            def ldx(n):
                i_ = tiles[n]
                load(xt[n % 2], xsrc[i_ * P:(i_ + 1) * P, :])
                load(rt[n % 2], CIN["c_rope"][i_ * P:(i_ + 1) * P, :])

            if tiles:
                ldx(0)
            for n, i in enumerate(tiles):
                if n + 1 < len(tiles):
                    ldx(n + 1)
                x, r = xt[n % 2], rt[n % 2]
                t0 = i * P
                act(junk, junk.ap, x, x.ap, AF.Square, accum=(ssx, ssx.ap))
                rstd_from_ss(ssx, rsx, D)
                act(hb, hb.ap, x, x.ap, AF.Copy, scale=rsx.ap, extra=[rsx])
                yield
                for kc in range(8):
                    tr(TBt, TBB[:, kc * 128:(kc + 1) * 128], hb, hb.ap[:, kc * 128:(kc + 1) * 128], IDB.ap, IDB)
                cp("act", hT, hT.ap.rearrange("p a b -> p (a b)"), TBt, TBB[:, 0:1024])
                yield
                for kc in range(8):
                    for (pt, c0, w) in ((Pcq, 1536, 384), (Pckv, 1920, 160), (Pcv, 2080, 512)):
                        mm(pt, pt.ap[:, 0:w], hT, hT.ap[:, kc, :], win, win.ap[:, kc, c0:c0 + w], kc == 0, kc == 7)
                yield
                act(junk, junk.ap[:, 0:384], Pcq, Pcq.ap[:, 0:384], AF.Square, accum=(ss2, ss2.ap[:, 0:1]))
                act(junk, junk.ap[:, 0:128], Pckv, Pckv.ap[:, 0:128], AF.Square, accum=(ss2, ss2.ap[:, 1:2]))
                act(rs2, rs2.ap[:, 0:1], ss2, ss2.ap[:, 0:1], AF.Sqrt, bias=EPST.ap, scale=1.0 / 384, extra=[EPST])
                act(rs2, rs2.ap[:, 1:2], ss2, ss2.ap[:, 1:2], AF.Sqrt, bias=EPST.ap, scale=1.0 / 128, extra=[EPST], partial=True)
                recip(rs2, rs2.ap)
                yield
                act(cqn, cqn.ap[:, 0:384], Pcq, Pcq.ap[:, 0:384], AF.Copy, scale=rs2.ap[:, 0:1], extra=[rs2])
                act(cqn, cqn.ap[:, 384:512], Pckv, Pckv.ap[:, 0:128], AF.Copy, scale=rs2.ap[:, 1:2], extra=[rs2], partial=True)
                yield
                kx = Pckv.ap[:, 128:160]
                tt("dve", tA2, tA2.ap, Pckv, kx, r, r.ap[:, 128:160], ALU.mult)
                tt("dve", tB2, tB2.ap[:, 0:16], Pckv, kx[:, 16:32], r, r.ap[:, 160:176], ALU.mult)
                tt("dve", tB2, tB2.ap[:, 16:32], Pckv, kx[:, 0:16], r, r.ap[:, 176:192], ALU.mult, partial=True)
                tt("dve", krf, krf.ap, tA2, tA2.ap, tB2, tB2.ap, ALU.add)
                yield
                act(junk, junk.ap[:, 0:32], krf, krf.ap, AF.Square, accum=(krss, krss.ap))
                cp("dve", kaug, kaug.ap[:, :, 64:96], krf, bc_mid(krf.ap, H), partial=True)
                act(sig, sig.ap, Pcv, Pcv.ap[:, 256:512], AF.Tanh, scale=0.5)
                yield
                ts("dve", sig, sig.ap, sig, sig.ap, 0.5, 0.5, ALU.mult, ALU.add)
                tt("dve", cb, cb.ap, Pcv, Pcv.ap[:, 0:256], sig, sig.ap, ALU.mult)
                yield
                for kc in range(8):
                    for (pt, c0) in ((M0, 0), (M1, 512), (M2, 1024)):
                        mm(pt, pt.ap, hT, hT.ap[:, kc, :], win, win.ap[:, kc, c0:c0 + 512], kc == 0, kc == 7)
                yield
                rope_ret(Pq, qrot, r, 1.0)
                yield
                rope_ret(Pk, krot, r, 0.125)
                yield
                cp("act", vb, vb.ap, Pv, Pv.ap[:, 0:384])
                act(gs, gs.ap, Pg, Pg.ap[:, 0:384], AF.Silu)
                yield
                for b_ in range(3):
                    tr(TBt, TBB[:, b_ * 128:(b_ + 1) * 128], qrot, qrot.ap[:, b_ * 128:(b_ + 1) * 128], IDB.ap, IDB)
                for b_ in range(3):
                    tr(TBt, TBB[:, (3 + b_) * 128:(4 + b_) * 128], krot, krot.ap[:, b_ * 128:(b_ + 1) * 128], IDB.ap, IDB)
                for b_ in range(2):
                    tr(TBt, TBB[:, (6 + b_) * 128:(7 + b_) * 128], cb, cb.ap[:, b_ * 128:(b_ + 1) * 128], IDB.ap, IDB)
                cp("act", s1, s1.ap.rearrange("p a b -> p (a b)"), TBt, TBB[:, 0:1024])
                yield
                for b_ in range(4):
                    tr(TBt, TBB[:, b_ * 128:(b_ + 1) * 128], cqn, cqn.ap[:, b_ * 128:(b_ + 1) * 128], IDB.ap, IDB)
                cp("dve", cqT, cqT.ap.rearrange("p a b -> p (a b)"), TBt, TBB[:, 0:512])
                yield
                for hh in range(2):
                    for kc in range(3):
                        mm(Uq[hh], Uq[hh].ap[:, 0:288], cqT, cqT.ap[:, kc, :], wuq, wuq.ap[:, kc, hh * 288:(hh + 1) * 288],
                           kc == 0, kc == 2)
                    mm(Ukv[hh], Ukv[hh].ap[:, 0:384], cqT, cqT.ap[:, 3, :], wukv, wukv.ap[:, hh * 384:(hh + 1) * 384],
                       True, True)
                yield
                for hh in range(2):
                    P3 = Uq[hh].ap[:, 0:288].rearrange("p (h d) -> p h d", d=96)
                    hs = slice(3 * hh, 3 * hh + 3)
                    act(qaug, qaug.ap[:, hs, 0:64], Uq[hh], P3[:, :, 0:64], AF.Copy, scale=SC, partial=True)
                    act(sq, sq.ap, Uq[hh], P3, AF.Square)
                    yield
                    stt("dve", tqa, tqa.ap, Uq[hh], P3[:, :, 64:96], SC, r, bc_mid(r.ap[:, 128:160], 3), ALU.mult, ALU.mult)
                    stt("dve", tqb, tqb.ap[:, :, 0:16], Uq[hh], P3[:, :, 80:96], SC, r, bc_mid(r.ap[:, 160:176], 3), ALU.mult, ALU.mult)
                    stt("dve", tqb, tqb.ap[:, :, 16:32], Uq[hh], P3[:, :, 64:80], SC, r, bc_mid(r.ap[:, 176:192], 3), ALU.mult,
                        ALU.mult, partial=True)
                    yield
                    tt("dve", qaug, qaug.ap[:, hs, 64:96], tqa, tqa.ap, tqb, tqb.ap, ALU.add, partial=True)
                    red(ssq, ssq.ap[:, hs], sq, sq.ap, partial=(hh == 1))
                    yield
                act(ssq, ssq.ap, ssq, ssq.ap, AF.Sqrt)
                yield
                ts("dve", qaug, qaug.ap[:, :, 96:97], ssq, ssq.ap.rearrange("p (h o) -> p h o", o=1), -SC, None, ALU.mult,
                   partial=True)
                for hh in range(2):
                    P3 = Ukv[hh].ap[:, 0:384].rearrange("p (h d) -> p h d", d=128)
                    hs = slice(3 * hh, 3 * hh + 3)
                    act(kaug, kaug.ap[:, hs, 0:64], Ukv[hh], P3[:, :, 0:64], AF.Copy, partial=True)
                    act(sq, sq.ap[:, :, 0:64], Ukv[hh], P3[:, :, 0:64], AF.Square)
                    yield
                    cp("dve", va, va.ap[:, hs, 0:64], Ukv[hh], P3[:, :, 64:128], partial=True)
                    red(kss, kss.ap[:, hs], sq, sq.ap[:, :, 0:64], partial=(hh == 1))
                    yield
                ts("dve", kss, kss.ap, kss, kss.ap, krss.ap, None, ALU.add, extra=[krss])
                tt("dve", kmax, kmax.ap, kmax, kmax.ap, kss, kss.ap, ALU.max)
                yield
                for hq in range(H):
                    tr(TBt, TBB[:, hq * 128:(hq + 1) * 128], qaug, qaug.ap[:, hq, :], IDB.ap, IDB)
                cp("act", qtm, qtm.ap.rearrange("p a b -> p (a b)"), TBt, TBB[:, 0:768])
                yield
                for hq in range(H):
                    tr(TBt, TBB[:, hq * 128:(hq + 1) * 128], kaug, kaug.ap[:, hq, :], IDB.ap, IDB)
                cp("dve", ktm, ktm.ap.rearrange("p a b -> p (a b)"), TBt, TBB[:, 0:768])
                yield
                store(sc["QTR"].rearrange("b p s -> p b s")[:, :, t0:t0 + P], s1, s1.ap[:, 0:3, :])
                store(sc["KTR"].rearrange("b p s -> p b s")[:, :, t0:t0 + P], s1, s1.ap[:, 3:6, :])
                store(ctv[:, :, 15 + t0:15 + t0 + P], s1, s1.ap[:, 6:8, :])
                store(sc["KR"][t0:t0 + P, :], krot)
                store(sc["VR"][t0:t0 + P, :], vb)
                store(sc["G"][t0:t0 + P, :], gs)
                store(sc["QTM"].rearrange("h p s -> p h s")[:, :, t0:t0 + P], qtm)
                store(sc["KTM"].rearrange("h p s -> p h s")[:, :, t0:t0 + P], ktm)
                store(sc["VM"].rearrange("h p n c -> p h n c")[:, :, i, :], va)
                yield

        gens = [stream(0), stream(1)]
        while gens:
            for g_ in list(gens):
                try:
                    next(g_)
                except StopIteration:
                    gens.remove(g_)
        S_.barrier()
        kmax = kmaxs[0]
        tt("dve", kmax, kmax.ap, kmax, kmax.ap, kmaxs[1], kmaxs[1].ap, ALU.max)
        tr(PB[0], PB[0].ap[0:H, 0:128], kmax, kmax.ap, IDF.ap, IDF)
        km1 = AF_.get(1)
        S_.op("dve", lambda e: e.tensor_reduce(out=km1.ap[0:H, :], in_=PB[0].ap[0:H, 0:128], axis=AX.X, op=ALU.max),
              reads=[PB[0]], writes=[km1])
        act(km1, km1.ap[0:H, :], km1, km1.ap[0:H, :], AF.Sqrt)
        dg = AF_.get(H)
        ts("dve", dg, dg.ap[0:H, :], IDF, IDF.ap[0:H, 0:H], km1.ap[0:H, 0:1], None, ALU.mult, extra=[km1])
        mm(PB[1], PB[1].ap[:, 0:H], SEL, SEL.ap[0:H, :], dg, dg.ap[0:H, :], True, True)
        tt("dve", KMX[sn], KMX[sn].ap, PB[1], PB[1].ap[:, 0:H], K64, K64.ap, ALU.add)
        S_.barrier()

    KMX = {sn: sb("kmx_" + sn, [H], F32) for (sn, _) in seqs}

    def phase_B1(l, sn, S):
        sc = SC_[sn]
        N = S // P
        sball = AB_.get(N, 3, 64)
        kt_ = [AB_.get(RW) for _ in range(2)]
        vt_ = [AB_.get(RW) for _ in range(2)]
        qT = [AB_.get(3, 128) for _ in range(2)]
        kT = [AB_.get(3, 128) for _ in range(2)]
        gt_ = [AF_.get(RW) for _ in range(2)]
        kd = AB_.get(RW)
        sm = AB_.get(H, 128)
        qf = AB_.get(3, 2, 128)
        qb = AB_.get(3, 2, 128)
        qz = [AB_.get(3, 2, 128) for _ in range(2)]
        for t_ in (qf, qb, qz[0], qz[1]):
            mset("dve", t_, t_.ap, 0.0)
        sf = AF_.get(3, 64)
        sbk = AF_.get(3, 64)
        sfb = [AB_.get(3, 64) for _ in range(2)]
        y1 = AF_.get(H, 64)
        y2 = AF_.get(H, 64)
        st = AF_.get(4, H)
        yo = AB_.get(RW)
        yts = [AB_.get(3, 128) for _ in range(2)]
        gnw = BROWS.ap[:, l, B_GN:B_GN + 384]

        def state_update(Pt, k_t, v_t, kdec, state, cdec, out_t, out_ap):
            tt("pool", kd, kd.ap, k_t, k_t.ap, kdec, kdec.ap, ALU.mult)
            for b_ in range(3):
                mm(Pt, Pt.ap[:, b_ * 128:(b_ + 1) * 128], kd, kd.ap[:, b_ * 128:(b_ + 1) * 128], v_t,
                   v_t.ap[:, b_ * 128:(b_ + 1) * 128], True, True)
            P3 = Pt.ap[:, 0:384].rearrange("p (b c) -> p b c", c=128)
            tt("dve", state, state.ap, state, state.ap, cdec, cdec.ap, ALU.mult)
            tt("dve", state, state.ap[0:64], state, state.ap[0:64], Pt, P3[0:64, :, 0:64], ALU.add, partial=True)
            tt("dve", state, state.ap[64:128], state, state.ap[64:128], Pt, P3[64:128, :, 64:128], ALU.add, partial=True)
            cp("act", out_t, out_ap, state, state.ap, partial=True)

        mset("dve", sbk, sbk.ap, 0.0)
        mset("dve", sf, sf.ap, 0.0)

        def ld1(c):
            load(kt_[c % 2], sc["KR"][c * P:(c + 1) * P, :])
            load(vt_[c % 2], sc["VR"][c * P:(c + 1) * P, :])

        if N > 1:
            ld1(N - 1)
        for c in range(N - 1, 0, -1):
            yield
            if c - 1 >= 1:
                ld1(c - 1)
            state_update(PB[4], kt_[c % 2], vt_[c % 2], KDB, sbk, CDB, sball, sball.ap[:, c - 1])

        if OPT.get('b1cut', 9) <= 1:
            return
        def ld2(c):
            load(kt_[c % 2], sc["KR"][c * P:(c + 1) * P, :])
            load(vt_[c % 2], sc["VR"][c * P:(c + 1) * P, :])
            load(qT[c % 2], sc["QTR"].rearrange("b p s -> p b s")[:, :, c * P:(c + 1) * P])
            load(kT[c % 2], sc["KTR"].rearrange("b p s -> p b s")[:, :, c * P:(c + 1) * P])
            load(gt_[c % 2], sc["G"][c * P:(c + 1) * P, :])

        ld2(0)
        for c in range(N):
            yield
            if c + 1 < N:
                ld2(c + 1)
            k_t, v_t, q_T, k_T, g_t = kt_[c % 2], vt_[c % 2], qT[c % 2], kT[c % 2], gt_[c % 2]
            q_Z = qz[c % 2]
            cp("dve", q_Z, q_Z.ap[0:64, :, 0, :], q_T, q_T.ap[0:64], partial=True)
            cp("dve", q_Z, q_Z.ap[64:128, :, 1, :], q_T, q_T.ap[64:128], partial=True)
            for hq in range(H):
                b_, pp = hq // 2, hq % 2
                bank = PB[hq // 3]
                o = (hq % 3) * 128
                mm(bank, bank.ap[:, o:o + 128], k_T, k_T.ap[:, b_, :], q_Z, q_Z.ap[:, b_, pp, :], True, True)
            yield
            for hb_ in range(2):
                tt("dve", sm, sm.ap[:, 3 * hb_:3 * hb_ + 3, :].rearrange("p a b -> p (a b)"), PB[hb_], PB[hb_].ap[:, 0:384],
                   MASK, MASK.ap[:, 3 * hb_:3 * hb_ + 3, :].rearrange("p a b -> p (a b)"), ALU.mult, partial=(hb_ == 1))
            if OPT.get('b1cut', 9) <= 2:
                continue
            yield
            if c > 0:
                tt("dve", qf, qf.ap[0:64, :, 0, :], q_T, q_T.ap[0:64], QDF, QDF.ap[0:64], ALU.mult, partial=True)
                tt("dve", qf, qf.ap[64:128, :, 1, :], q_T, q_T.ap[64:128], QDF, QDF.ap[64:128], ALU.mult, partial=True)
            if c < N - 1:
                tt("dve", qb, qb.ap[0:64, :, 0, :], q_T, q_T.ap[0:64], QDB, QDB.ap[0:64], ALU.mult, partial=True)
                tt("dve", qb, qb.ap[64:128, :, 1, :], q_T, q_T.ap[64:128], QDB, QDB.ap[64:128], ALU.mult, partial=True)
            yield
            Py = PB[2]
            sfc = sfb[c % 2]
            for hq in range(H):
                b_, pp = hq // 2, hq % 2
                ps_ = slice(pp * 64, (pp + 1) * 64)
                last_i = not (c > 0 or c < N - 1)
                mm(Py, Py.ap[:, hq * 64:(hq + 1) * 64], sm, sm.ap[:, hq, :], v_t, v_t.ap[:, hq * 64:(hq + 1) * 64], True, last_i)
                if c > 0:
                    mm(Py, Py.ap[:, hq * 64:(hq + 1) * 64], qf, qf.ap[:, b_, pp, :], sfc, sfc.ap[:, b_, :], False, not (c < N - 1))
                if c < N - 1:
                    mm(Py, Py.ap[:, hq * 64:(hq + 1) * 64], qb, qb.ap[:, b_, pp, :], sball, sball.ap[:, c, b_, :], False, True)
            if OPT.get('b1cut', 9) <= 3:
                continue
            yield
            if c < N - 1:
                nx = sfb[(c + 1) % 2]
                state_update(PB[3], k_t, v_t, KDF, sf, CDF, nx, nx.ap)
            if OPT.get('b1cut', 9) <= 4:
                continue
            yield
            Py3 = Py.ap[:, 0:384].rearrange("p (h v) -> p h v", v=64)
            red(st, st.ap[:, 0, :], Py, Py3)
            act(y1, y1.ap, Py, Py3, AF.Square)
            red(st, st.ap[:, 1, :], y1, y1.ap, partial=True)
            yield
            ts("dve", st, st.ap[:, 0:2, :], st, st.ap[:, 0:2, :], 1.0 / 64, None, ALU.mult)
            tt("dve", st, st.ap[:, 2, :], st, st.ap[:, 0, :], st, st.ap[:, 0, :], ALU.mult, partial=True)
            tt("dve", st, st.ap[:, 3, :], st, st.ap[:, 1, :], st, st.ap[:, 2, :], ALU.subtract, partial=True)
            ts("dve", st, st.ap[:, 3, :], st, st.ap[:, 3, :], 0.0, None, ALU.max, partial=True)
            rsq(st, st.ap[:, 3, :], st, st.ap[:, 3, :], 1.0, partial=True)
            if OPT.get('b1cut', 9) <= 5:
                continue
            yield
            tt("dve", y2, y2.ap, Py, Py3, st, bc_last(st.ap[:, 0, :], 64), ALU.subtract)
            tt("pool", y1, y1.ap, y2, y2.ap, st, bc_last(st.ap[:, 3, :], 64), ALU.mult)
            yield
            y1f = y1.ap.rearrange("p h v -> p (h v)")
            y2f = y2.ap.rearrange("p h v -> p (h v)")
            tt("pool", y2, y2f, y1, y1f, BROWS, gnw, ALU.mult)
            tt("dve", yo, yo.ap, y2, y2f, g_t, g_t.ap, ALU.mult)
            if OPT.get('b1cut', 9) <= 6:
                continue
            yield
            for b_ in range(3):
                tr(PB[7], PB7B[:, b_ * 128:(b_ + 1) * 128], yo, yo.ap[:, b_ * 128:(b_ + 1) * 128], IDB.ap, IDB)
            ys = yts[c % 2]
            cp("act", ys, ys.ap.rearrange("p a b -> p (a b)"), PB[7], PB7B[:, 0:384])
            store(sc["YT"].rearrange("k p s -> p k s")[:, 0:3, c * P:(c + 1) * P], ys)

    def phase_B2(l, sn, S):
        sc = SC_[sn]
        NB = S // P
        QC = 512
        NQ = S // QC
        S_.new_phase()
        AF_.reset()
        AB_.reset()
        KT = [AB_.get(S) for _ in range(2)]
        VH = [AB_.get(NB, 65) for _ in range(2)]
        QTc = [AB_.get(QC) for _ in range(2)]
        PT = [AB_.get(QC) for _ in range(4)]
        rden = AF_.get(QC)
        mset("dve", rden, rden.ap, 0.0)
        bcs = AF_.get(QC)
        yo = [AB_.get(QC) for _ in range(2)]
        kmx = KMX[sn]

        def ldh(hq):
            load(KT[hq % 2], sc["KTM"][hq, :, :])
            load(VH[hq % 2], sc["VM"][hq, :, :, :])

        it = [0]

        def ldq(hq, qi):
            load(QTc[it[0] % 2], sc["QTM"][hq, :, qi * QC:(qi + 1) * QC])

        ldh(0)
        ldq(0, 0)
        for hq in range(H):
            if hq + 1 < H:
                ldh(hq + 1)
            K_, V_ = KT[hq % 2], VH[hq % 2]
            ts("dve", K_, K_.ap[64:128, :], K_, K_.ap[64:128, :], kmx.ap[64:128, hq:hq + 1], None, ALU.mult, extra=[kmx])
            for qi in range(NQ):
                cur = it[0]
                it[0] += 1
                nh, nq = (hq, qi + 1) if qi + 1 < NQ else (hq + 1, 0)
                if nh < H:
                    ldq(nh, nq)
                Q_ = QTc[cur % 2]
                Po = PB[4 + cur % 2]

                def pv(n):
                    mm(Po, Po.ap[0:65, :], V_, V_.ap[:, n, :], PT[n % 4], PT[n % 4].ap, n == 0, n == NB - 1)

                for n in range(NB):
                    ps_ = PB[n % 4]
                    mm(ps_, ps_.ap, K_, K_.ap[:, n * P:(n + 1) * P], Q_, Q_.ap, True, True)
                    act(PT[n % 4], PT[n % 4].ap, ps_, ps_.ap, AF.Exp)
                    if n >= 3:
                        pv(n - 3)
                for n in range(max(NB - 3, 0), NB):
                    pv(n)
                S_.op("dve", lambda e, Po=Po: e.reciprocal(out=rden.ap[64:65, :], in_=Po.ap[64:65, :]), reads=[Po], partial=[rden])
                mm(PB[6], PB[6].ap[0:64, :], SELB, SELB.ap, rden, rden.ap, True, True)
                cp("act", bcs, bcs.ap[0:64, :], PB[6], PB[6].ap[0:64, :])
                y_ = yo[cur % 2]
                tt("dve", y_, y_.ap[0:64, :], Po, Po.ap[0:64, :], bcs, bcs.ap[0:64, :], ALU.mult)
                kc = 3 + hq // 2
                pp = hq % 2
                store(sc["YT"][kc, pp * 64:(pp + 1) * 64, qi * QC:(qi + 1) * QC], y_, y_.ap[0:64, :])
        S_.barrier()

    def phase_B3(l, sn, S, sid, sh):
        sc = SC_[sn]
        NT = S // P
        if "diag" not in sh:
            sh["diag"] = AB_.get(62, 128)
            sh["wpw"] = AB_.get(2, 256)
            load(sh["wpw"], WS[l]["WPW"][:, :, :])
            for c in range(2):
                for j in range(31):
                    col = C_DW + 31 * c + j
                    ts("dve", sh["diag"], sh["diag"].ap[:, c * 31 + j, :], IDB, IDB.ap, COLS.ap[:, l, col:col + 1], None,
                       ALU.mult, extra=[COLS], partial=True)
        diag, wpw = sh["diag"], sh["wpw"]
        cw = [AB_.get(2, 158) for _ in range(2)]
        x1 = AF_.get(256)
        x2 = AF_.get(256)
        st = AF_.get(8)
        junk = AF_.get(256)
        cs = AB_.get(256)
        c2T = AB_.get(2, 128)
        ys = [AB_.get(2, 128) for _ in range(2)]
        ctv = sc["CT"].rearrange("c p s -> p c s")
        dwb = BROWS.ap[:, l, B_DWB:B_DWB + 256]
        lnw = BROWS.ap[:, l, B_LNW:B_LNW + 256]
        lnb = BROWS.ap[:, l, B_LNB:B_LNB + 256]
        tiles = list(range(sid, NT, 2))
        BK = PB[5 + sid]
        if tiles:
            load(cw[0], ctv[:, :, tiles[0] * P:tiles[0] * P + 158])
        for n, i in enumerate(tiles):
            yield
            if n + 1 < len(tiles):
                i2 = tiles[n + 1]
                load(cw[(n + 1) % 2], ctv[:, :, i2 * P:i2 * P + 158])
            w_ = cw[n % 2]
            for c in range(2):
                if c == 1:
                    yield
                for j in range(31):
                    mm(BK, BK.ap[:, c * 128:(c + 1) * 128], w_, w_.ap[:, c, j:j + 128], diag, diag.ap[:, c * 31 + j, :],
                       j == 0, j == 30)
            yield
            tt("dve", x1, x1.ap, BK, BK.ap[:, 0:256], BROWS, dwb, ALU.add)
            act(junk, junk.ap, x1, x1.ap, AF.Square, accum=(st, st.ap[:, 1:2]))
            red(st, st.ap[:, 0:1], x1, x1.ap, partial=True)
            yield
            ts("dve", st, st.ap[:, 0:2], st, st.ap[:, 0:2], 1.0 / 256, None, ALU.mult)
            tt("dve", st, st.ap[:, 2:3], st, st.ap[:, 0:1], st, st.ap[:, 0:1], ALU.mult, partial=True)
            tt("dve", st, st.ap[:, 3:4], st, st.ap[:, 1:2], st, st.ap[:, 2:3], ALU.subtract, partial=True)
            ts("dve", st, st.ap[:, 3:4], st, st.ap[:, 3:4], 0.0, None, ALU.max, partial=True)
            rsq(st, st.ap[:, 3:4], st, st.ap[:, 3:4], 1.0, partial=True)
            yield
            ts("dve", x2, x2.ap, x1, x1.ap, st.ap[:, 0:1], st.ap[:, 3:4], ALU.subtract, ALU.mult, extra=[st])
            tt("pool", x1, x1.ap, x2, x2.ap, BROWS, lnw, ALU.mult)
            tt("pool", x2, x2.ap, x1, x1.ap, BROWS, lnb, ALU.add)
            yield
            act(cs, cs.ap, x2, x2.ap, AF.Silu)
            for c in range(2):
                tr(PB[7], PB7B[:, c * 128:(c + 1) * 128], cs, cs.ap[:, c * 128:(c + 1) * 128], IDB.ap, IDB)
            cp("dve", c2T, c2T.ap.rearrange("p a b -> p (a b)"), PB[7], PB7B[:, 0:256])
            yield
            for oc in range(2):
                for kc in range(2):
                    mm(BK, BK.ap[:, 256 + oc * 128:256 + (oc + 1) * 128], wpw, wpw.ap[:, kc, oc * 128:(oc + 1) * 128], c2T,
                       c2T.ap[:, kc, :], kc == 0, kc == 1)
            yield
            y_ = ys[n % 2]
            for oc in range(2):
                act(y_, y_.ap[:, oc, :], BK, BK.ap[:, 256 + oc * 128:256 + (oc + 1) * 128], AF.Identity,
                    bias=COLS.ap[:, l, C_PWB + oc:C_PWB + oc + 1], scale=1.0, extra=[COLS], partial=(oc == 1))
            store(sc["YT"].rearrange("k p s -> p k s")[:, 6:8, i * P:(i + 1) * P], y_)

    def phase_C1(l, sn, S, xsrc):
        sc = SC_[sn]
        NT = S // P
        S_.new_phase()
        AF_.reset()
        AB_.reset()
        wo = AB_.get(8, D)
        load(wo, WS[l]["WOUT"][:, :, :])
        yt = [AB_.get(8, 128) for _ in range(2)]
        xt = [AF_.get(D) for _ in range(2)]
        xo = [AF_.get(D) for _ in range(2)]
        ytv = sc["YT"].rearrange("k p s -> p k s")

        def ld(i):
            load(yt[i % 2], ytv[:, :, i * P:(i + 1) * P])
            load(xt[i % 2], xsrc[i * P:(i + 1) * P, :])

        ld(0)
        for i in range(NT):
            if i + 1 < NT:
                ld(i + 1)
            y_ = yt[i % 2]
            banks = (PB[0], PB[1]) if i % 2 == 0 else (PB[2], PB[3])
            for cg in range(2):
                for kc in range(8):
                    mm(banks[cg], banks[cg].ap, y_, y_.ap[:, kc, :], wo, wo.ap[:, kc, cg * 512:(cg + 1) * 512], kc == 0, kc == 7)
            o_ = xo[i % 2]
            for cg in range(2):
                tt("dve", o_, o_.ap[:, cg * 512:(cg + 1) * 512], banks[cg], banks[cg].ap, xt[i % 2],
                   xt[i % 2].ap[:, cg * 512:(cg + 1) * 512], ALU.add, partial=(cg == 1))
            store(sc["X1"][i * P:(i + 1) * P, :], o_)
        S_.barrier()

    def phase_C2(l, sn, S, last):
        sc = SC_[sn]
        ws = WS[l]
        TT = 512
        NTT = S // TT
        S_.new_phase()
        AF_.reset()
        AB_.reset()
        wd = AB_.get(NJ, D)
        load(wd, ws["WDN"][:, :, :])
        wu = [AB_.get(8, 256) for _ in range(2)]
        x1 = [AF_.get(D) for _ in range(4)]
        xh = AF_.get(D)
        ssx = AF_.get(1)
        rsx = AF_.get(1)
        junk = AB_.get(D)
        hb = AB_.get(D)
        h2T = AB_.get(8, 514)
        accg2 = [AF_.get(512) for _ in range(2)]
        accu2 = [AF_.get(512) for _ in range(2)]
        sg2 = [AF_.get(512) for _ in range(2)]
        actT = AB_.get(NJ, 512)
        xo = [AF_.get(D) for _ in range(2)]
        x1s = sc["X1"]
        wi = [0]

        def ldw(j):
            load(wu[wi[0] % 2], ws["WUP"][j, :, :, :])
            wi[0] += 1

        def conv_branch(Pt, Ph, hoff, ch, acc):
            w0 = COLS.ap[:, l, C_FCW + ch:C_FCW + ch + 1]
            w1 = COLS.ap[:, l, C_FCW + 44 + ch:C_FCW + 44 + ch + 1]
            w2 = COLS.ap[:, l, C_FCW + 88 + ch:C_FCW + 88 + ch + 1]
            bb = COLS.ap[:, l, C_FCB + ch:C_FCB + ch + 1]
            act(acc, acc.ap, Pt, Pt.ap, AF.Identity, bias=bb, scale=w1, extra=[COLS])
            stt("dve", acc, acc.ap[:, 1:512], Pt, Pt.ap[:, 0:511], w0, acc, acc.ap[:, 1:512], ALU.mult, ALU.add, extra=[COLS])
            stt("dve", acc, acc.ap[:, 0:511], Pt, Pt.ap[:, 1:512], w2, acc, acc.ap[:, 0:511], ALU.mult, ALU.add, extra=[COLS])
            stt("dve", acc, acc.ap[:, 0:1], Ph, Ph.ap[:, hoff:hoff + 1], w0, acc, acc.ap[:, 0:1], ALU.mult, ALU.add, extra=[COLS])
            stt("dve", acc, acc.ap[:, 511:512], Ph, Ph.ap[:, hoff + 1:hoff + 2], w2, acc, acc.ap[:, 511:512], ALU.mult, ALU.add,
                extra=[COLS])

        ldw(0)
        for ti in range(NTT):
            t0 = ti * TT
            for s in range(4):
                load(x1[s], x1s[t0 + s * P:t0 + (s + 1) * P, :])
            load(xh, (x1s[t0 - 1:t0, :] if t0 > 0 else ZROW[:, :]), out_ap=xh.ap[0:1, :])
            load(xh, (x1s[t0 + TT:t0 + TT + 1, :] if t0 + TT < S else ZROW[:, :]), out_ap=xh.ap[1:2, :], partial=True)
            for s in range(5):
                xs = x1[s] if s < 4 else xh
                nr = P if s < 4 else 2
                act(junk, junk.ap[0:nr, :], xs, xs.ap[0:nr, :], AF.Square, accum=(ssx, ssx.ap[0:nr, :]))
                rsq(rsx, rsx.ap[0:nr, :], ssx, ssx.ap[0:nr, :], 1.0 / D)
                act(hb, hb.ap[0:nr, :], xs, xs.ap[0:nr, :], AF.Copy, scale=rsx.ap[0:nr, 0:1], extra=[rsx])
                for kc in range(8):
                    tr(PB[7], PB7B[:, kc * 128:kc * 128 + nr], hb, hb.ap[0:nr, kc * 128:(kc + 1) * 128], IDB.ap[0:nr, 0:nr], IDB)
                src = PB7B[:, 0:1024].rearrange("p (k t) -> p k t", t=128)[:, :, 0:nr]
                cp("act", h2T, h2T.ap[:, :, s * P:s * P + nr], PB[7], src, partial=(s > 0))
            for j in range(NJ):
                if not (ti == NTT - 1 and j == NJ - 1):
                    ldw((j + 1) % NJ)
                w_ = wu[(ti * NJ + j) % 2]
                Pg_, Pu_ = PB[j % 2], PB[2 + j % 2]
                Ph = PB[4] if j % 2 == 0 else PB[7]
                ho = 0
                accg, accu, sg = accg2[j % 2], accu2[j % 2], sg2[j % 2]
                for (Pt, hoff, c0) in ((Pg_, ho, 0), (Pu_, ho + 2, 128)):
                    for kc in range(8):
                        mm(Pt, Pt.ap, w_, w_.ap[:, kc, c0:c0 + 128], h2T, h2T.ap[:, kc, 0:512], kc == 0, kc == 7)
                    for kc in range(8):
                        mm(Ph, Ph.ap[:, hoff:hoff + 2], w_, w_.ap[:, kc, c0:c0 + 128], h2T, h2T.ap[:, kc, 512:514], kc == 0, kc == 7)
                conv_branch(Pg_, Ph, ho, j, accg)
                conv_branch(Pu_, Ph, ho + 2, NJ + j, accu)
                act(sg, sg.ap, accg, accg.ap, AF.Silu)
                tt("pool!", actT, actT.ap[:, j, :], sg, sg.ap, accu, accu.ap, ALU.mult, partial=(j > 0))
            for s in range(4):
                o_ = xo[s % 2]
                for cg in range(2):
                    Pd = PB[5 + cg]
                    for j in range(NJ):
                        mm(Pd, Pd.ap, actT, actT.ap[:, j, s * P:(s + 1) * P], wd, wd.ap[:, j, cg * 512:(cg + 1) * 512],
                           j == 0, j == NJ - 1)
                    tt("dve", o_, o_.ap[:, cg * 512:(cg + 1) * 512], Pd, Pd.ap, x1[s], x1[s].ap[:, cg * 512:(cg + 1) * 512],
                       ALU.add, partial=(cg == 1))
                r0 = t0 + s * P
                if not last:
                    store(sc["X2"][r0:r0 + P, :], o_)
                else:
                    act(junk, junk.ap, o_, o_.ap, AF.Square, accum=(ssx, ssx.ap))
                    rstd_from_ss(ssx, rsx, D)
                    stt("dve", o_, o_.ap, o_, o_.ap, rsx.ap, FNW, FNW.ap, ALU.mult, ALU.mult, extra=[rsx])
                    store(YOUT[sn][r0:r0 + P, :], o_)
        S_.barrier()

    done = False
    for l in range(depth):
        for (sn, S) in seqs:
            xsrc = XIN[sn] if l == 0 else SC_[sn]["X2"]
            for ph in ("A", "B13", "B2", "C1", "C2"):
                if ph == "A":
                    phase_A(l, sn, S, xsrc)
                elif ph == "B13":
                    S_.new_phase()
                    AF_.reset()
                    AB_.reset()
                    sh_ = {}
                    gens = [phase_B1(l, sn, S), phase_B3(l, sn, S, 0, sh_), phase_B3(l, sn, S, 1, sh_)]
                    while gens:
                        for g_ in list(gens):
                            try:
                                next(g_)
                            except StopIteration:
                                gens.remove(g_)
                    S_.barrier()
                elif ph == "B2":
                    phase_B2(l, sn, S)
                elif ph == "C1":
                    phase_C1(l, sn, S, xsrc)
                else:
                    phase_C2(l, sn, S, l == depth - 1)
                if stop_after == (l, sn, ph):
                    done = True
                    break
            if done:
                break
        if done:
            break
    S_.emit()
    return nc


_CACHE = {}
OPT = {}


def kernel(**inputs):
    SP_, SS_ = 8192, 2048
    depth = 2
    ncores = 8
    key = "main"
    if key not in _CACHE:
        _CACHE[key] = build([("p", SP_), ("s", SS_)], depth, SP_)
    nc = _CACHE[key]
    hc = host_consts(SP_)
    xp = np.asarray(inputs["x_prompt"], dtype=np.float32)
    xs = np.asarray(inputs["x_sample"], dtype=np.float32)
    base = {n: np.ascontiguousarray(np.asarray(inputs[n], dtype=np.float32)) for n in WNAMES}
    base.update(hc)
    in_maps = []
    for c in range(ncores):
        m = dict(base)
        m["x_p"] = np.ascontiguousarray(xp[c])
        m["x_s"] = np.ascontiguousarray(xs[c])
        in_maps.append(m)
    res = run_bass_kernel_spmd(nc, in_maps, core_ids=list(range(ncores)))
    yp = np.stack([np.asarray(res.results[c]["y_p"], dtype=np.float32) for c in range(ncores)], axis=0)
    ys = np.stack([np.asarray(res.results[c]["y_s"], dtype=np.float32) for c in range(ncores)], axis=0)
    return (yp, ys)
```

```python
import numpy as np
import ml_dtypes
import concourse.bass as bass
import concourse.mybir as mybir
from concourse.bass_utils import run_bass_kernel_spmd

F32 = mybir.dt.float32
BF16 = mybir.dt.bfloat16
AF = mybir.ActivationFunctionType
ALU = mybir.AluOpType
AX = mybir.AxisListType

P = 128
D = 1024
H = 6
RW = 384
INC = 2592
DFF = 2816
NJ = 22
EPS = 1e-6
SC = float(96 ** -0.5)
C_AN, C_QN, C_KVN, C_FN, C_FCW, C_FCB, C_PWB, C_DW, NCOLS = 0, 8, 11, 12, 20, 152, 196, 198, 260
B_GN, B_DWB, B_LNW, B_LNB, NBROW = 0, 384, 640, 896, 1152


class Buf:
    __slots__ = ("w", "r")

    def __init__(self):
        self.w = {}
        self.r = {}


class Tile:
    def __init__(self, ap):
        self.ap = ap
        self.buf = Buf()
        self.ds = None


class DSem:
    def __init__(self, name, sem):
        self.name = name
        self.sem = sem
        self.val = 0


class Sched:
    ENG = ("pe", "act", "dve", "pool", "sp")

    def __init__(self, nc):
        self.nc = nc
        self.ops = {n: [] for n in self.ENG}
        self.esem = {n: nc.alloc_semaphore(name="es_" + n) for n in ("pe", "act", "dve", "pool")}
        self.ecnt = {n: 0 for n in self.esem}
        self.seen = {n: {} for n in self.ENG}
        self.dpool = []
        self.dnext = 0

    def new_phase(self):
        self.dnext = 0

    def dsem(self):
        if self.dnext >= len(self.dpool):
            nm = "ds%d" % len(self.dpool)
            self.dpool.append(DSem(nm, self.nc.alloc_semaphore(name=nm)))
        d = self.dpool[self.dnext]
        self.dnext += 1
        return d

    def op(self, eng, fn, reads=(), writes=(), partial=(), dma=None):
        if eng == "pool" and dma is None and not OPT.get('usepool'):
            eng = "dve"
        if eng == "pool!":
            eng = "pool"
        need = {}

        def add(d):
            for k, sv in d.items():
                if k not in need or need[k][1] < sv[1]:
                    need[k] = sv

        for t in reads:
            add(t.buf.w)
            if getattr(t, 'psum', False) and eng != 'pe':
                for k, sv in t.buf.r.items():
                    if k != 'es_' + eng and (k not in need or need[k][1] < sv[1]):
                        need[k] = sv
        for t in writes:
            add(t.buf.w)
            add(t.buf.r)
        for t in partial:
            add(t.buf.r)
        waits = []
        seen = self.seen[eng]
        for k, (s, v) in need.items():
            if eng == "pe" and k == "es_pe":
                continue
            if seen.get(k, 0) >= v:
                continue
            seen[k] = v
            waits.append((s, v))
        if dma is None:
            sem = self.esem[eng]
            self.ecnt[eng] += 1
            val = self.ecnt[eng]
            inc = 1
            key = "es_" + eng
        else:
            dma.val += 16
            sem, val, inc, key = dma.sem, dma.val, 16, dma.name
        for t in writes:
            t.buf.w = {key: (sem, val)}
            t.buf.r = {}
        for t in partial:
            t.buf.w[key] = (sem, val)
        for t in reads:
            t.buf.r[key] = (sem, val)
        self.ops[eng].append((fn, waits, sem, inc))

    def barrier(self):
        cur = {}
        for n, s in self.esem.items():
            cur["es_" + n] = (s, self.ecnt[n])
        for d in self.dpool:
            cur[d.name] = (d.sem, d.val)
        for eng in self.ENG:
            waits = []
            for k, (s, v) in cur.items():
                if v > 0 and self.seen[eng].get(k, 0) < v:
                    self.seen[eng][k] = v
                    waits.append((s, v))
            if waits:
                self.ops[eng].append((None, waits, None, 0))

    def emit(self):
        nc = self.nc
        with nc.Block() as block:
            for name, deco in (("sp", block.sync), ("act", block.scalar), ("pe", block.tensor),
                               ("dve", block.vector), ("pool", block.gpsimd)):
                def body(e, name=name):
                    for fn, waits, sem, inc in self.ops[name]:
                        if fn is None or not OPT.get('attach'):
                            for s, v in waits:
                                e.wait_ge(s, v)
                            if fn is not None:
                                fn(e).then_inc(sem, inc)
                        else:
                            for s, v in waits[:-1]:
                                e.wait_ge(s, v)
                            ins = fn(e)
                            if waits:
                                ins._wait_ge(waits[-1][0], waits[-1][1])
                            ins.then_inc(sem, inc)
                deco(body)


class Arena:
    def __init__(self, nc, name, nelem, dt):
        self.t = nc.alloc_sbuf_tensor(name, [P, nelem], dt)
        self.cap = nelem
        self.off = 0
        self.al = 16 if dt == F32 else 32

    def reset(self):
        self.off = 0

    def get(self, *shape):
        n = int(np.prod(shape))
        n_al = (n + self.al - 1) // self.al * self.al
        assert self.off + n_al <= self.cap, ("arena overflow", self.off, n_al, self.cap)
        ap = self.t[:, self.off:self.off + n]
        self.off += n_al
        if len(shape) == 2:
            ap = ap.rearrange("p (a b) -> p a b", b=shape[1])
        elif len(shape) == 3:
            ap = ap.rearrange("p (a b c) -> p a b c", b=shape[1], c=shape[2])
        return Tile(ap)


def bc_mid(ap, reps):
    a = [list(x) for x in ap.ap]
    return bass.AP(ap.tensor, ap.offset, [a[0], [0, reps]] + a[1:])


def bc_last(ap, reps):
    a = [list(x) for x in ap.ap]
    return bass.AP(ap.tensor, ap.offset, a + [[0, reps]])


def host_consts(smax):
    C = 128
    pos = np.arange(smax, dtype=np.float32)

    def tab(half):
        fr = (1.0 / (np.float32(10000.0) ** (np.arange(half, dtype=np.float32) / np.float32(half)))).astype(np.float32)
        ang = (pos[:, None] * fr[None, :]).astype(np.float32)
        c = np.cos(ang).astype(np.float32)
        s = np.sin(ang).astype(np.float32)
        return np.concatenate([c, c, -s, s], axis=1)

    rope = np.concatenate([tab(32), tab(16)], axis=1).astype(np.float32)
    h = np.arange(H, dtype=np.float64)
    gf = 1.0 - 2.0 ** (-5.0 - h)
    gb = 1.0 - 2.0 ** (-5.5 - h)
    idx = np.arange(C, dtype=np.float64)
    dif = idx[None, :] - idx[:, None]
    mk = np.zeros((C, H, C))
    for hh in range(H):
        mk[:, hh, :] = np.where(dif >= 0, gf[hh] ** np.maximum(dif, 0), gb[hh] ** np.maximum(-dif, 0))
    qdf = np.zeros((C, 3, C))
    qdb = np.zeros((C, 3, C))
    cdf = np.zeros((C, 3, 64))
    cdb = np.zeros((C, 3, 64))
    for hh in range(H):
        b, pp = hh // 2, hh % 2
        qdf[pp * 64:(pp + 1) * 64, b, :] = (gf[hh] ** (idx + 1.0))[None, :]
        qdb[pp * 64:(pp + 1) * 64, b, :] = (gb[hh] ** (C - idx))[None, :]
        cdf[pp * 64:(pp + 1) * 64, b, :] = gf[hh] ** C
        cdb[pp * 64:(pp + 1) * 64, b, :] = gb[hh] ** C
    kdf = np.zeros((C, H, 64))
    kdb = np.zeros((C, H, 64))
    for hh in range(H):
        kdf[:, hh, :] = (gf[hh] ** (C - 1.0 - idx))[:, None]
        kdb[:, hh, :] = (gb[hh] ** idx)[:, None]
    sel = np.zeros((8, 128), np.float32)
    sel[:, 96] = 1.0
    k64 = np.zeros((128, 6), np.float32)
    k64[64:96, :] = 1.0
    selb = np.zeros((128, 64), np.float32)
    selb[64, :] = 1.0
    return {
        "c_selb": selb,
        "c_rope": rope,
        "c_mask": mk.reshape(C, H * C).astype(np.float32),
        "c_qdf": qdf.reshape(C, 384).astype(np.float32),
        "c_qdb": qdb.reshape(C, 384).astype(np.float32),
        "c_cdf": cdf.reshape(C, 192).astype(np.float32),
        "c_cdb": cdb.reshape(C, 192).astype(np.float32),
        "c_kdf": kdf.reshape(C, 384).astype(np.float32),
        "c_kdb": kdb.reshape(C, 384).astype(np.float32),
        "c_idb": np.eye(128).astype(ml_dtypes.bfloat16),
        "c_idf": np.eye(128).astype(np.float32),
        "c_sel": sel,
        "c_k64": k64,
    }


WNAMES = ["attn_norm_w", "w_in", "ret_gn_w", "mla_q_norm_w", "mla_w_uq", "mla_kv_norm_w", "mla_w_ukv",
          "conv_dw_w", "conv_dw_b", "conv_ln_w", "conv_ln_b", "conv_pw_w", "conv_pw_b", "w_out",
          "ffn_norm_w", "w_up", "ffn_conv_w", "ffn_conv_b", "w_down", "final_norm_w"]
WSHAPES = {
    "attn_norm_w": [D], "w_in": [D, INC], "ret_gn_w": [RW], "mla_q_norm_w": [384], "mla_w_uq": [384, 576],
    "mla_kv_norm_w": [128], "mla_w_ukv": [128, 768], "conv_dw_w": [31, 256], "conv_dw_b": [256],
    "conv_ln_w": [256], "conv_ln_b": [256], "conv_pw_w": [256, 256], "conv_pw_b": [256], "w_out": [D, D],
    "ffn_norm_w": [D], "w_up": [D, 2 * DFF], "ffn_conv_w": [3, 2 * DFF], "ffn_conv_b": [2 * DFF], "w_down": [DFF, D],
}


def build(seqs, depth, smax, dbg=None, stop_after=None):
    nc = bass.Bass("TRN2", target_bir_lowering=False)
    S_ = Sched(nc)
    dbg = dbg or []

    def dram(name, shape, dt, kind="Internal"):
        if name in dbg:
            kind = "ExternalOutput"
        return nc.dram_tensor(name, list(shape), dt, kind=kind).ap()

    W = {}
    for n in WNAMES:
        shp = ([depth] + WSHAPES[n]) if n != "final_norm_w" else [D]
        W[n] = nc.dram_tensor(n, shp, F32, kind="ExternalInput").ap()
    hc = host_consts(smax)
    CIN = {k: nc.dram_tensor(k, list(v.shape), BF16 if v.dtype == ml_dtypes.bfloat16 else F32,
                             kind="ExternalInput").ap() for k, v in hc.items()}
    XIN, YOUT, SC_ = {}, {}, {}
    for (sn, S) in seqs:
        XIN[sn] = nc.dram_tensor("x_" + sn, [S, D], F32, kind="ExternalInput").ap()
        YOUT[sn] = nc.dram_tensor("y_" + sn, [S, D], F32, kind="ExternalOutput").ap()
        NB = S // P
        SC_[sn] = dict(
            QTR=dram("QTR_" + sn, [3, P, S], BF16), KTR=dram("KTR_" + sn, [3, P, S], BF16),
            KR=dram("KR_" + sn, [S, RW], BF16), VR=dram("VR_" + sn, [S, RW], BF16), G=dram("G_" + sn, [S, RW], F32),
            QTM=dram("QTM_" + sn, [H, P, S], BF16), KTM=dram("KTM_" + sn, [H, P, S], BF16),
            VM=dram("VM_" + sn, [H, P, NB, 65], BF16), CT=dram("CT_" + sn, [2, P, S + 30], BF16),
            YT=dram("YT_" + sn, [8, P, S], BF16), X1=dram("X1_" + sn, [S, D], F32), X2=dram("X2_" + sn, [S, D], F32),
        )
    WS = []
    for l in range(depth):
        WS.append(dict(
            WIN=dram("WIN%d" % l, [P, 8, INC], BF16), WUQ=dram("WUQ%d" % l, [P, 3, 576], BF16),
            WUKV=dram("WUKV%d" % l, [P, 768], BF16), WPW=dram("WPW%d" % l, [P, 2, 256], BF16),
            WOUT=dram("WOUT%d" % l, [P, 8, D], BF16), WUP=dram("WUP%d" % l, [NJ, P, 8, 256], BF16),
            WDN=dram("WDN%d" % l, [P, NJ, D], BF16)))
    ZROW = dram("ZROW", [1, D], F32)

    def sb(name, shape, dt):
        t = nc.alloc_sbuf_tensor(name, [P] + list(shape), dt)
        idx = (slice(None),) * (len(shape) + 1)
        return Tile(t[idx])

    IDB = sb("idb", [128], BF16)
    IDF = sb("idf", [128], F32)
    COLS = sb("cols", [depth, NCOLS], F32)
    BROWS = sb("brows", [depth, NBROW], F32)
    FNW = sb("fnw", [D], F32)
    MASK = sb("mask", [H, 128], F32)
    QDF = sb("qdf", [3, 128], F32)
    QDB = sb("qdb", [3, 128], F32)
    CDF = sb("cdf", [3, 64], F32)
    CDB = sb("cdb", [3, 64], F32)
    KDF = sb("kdf", [RW], F32)
    KDB = sb("kdb", [RW], F32)
    SEL = sb("sel", [128], F32)
    K64 = sb("k64", [6], F32)
    SELB = sb("selb", [64], F32)
    ZT = sb("zt", [D], F32)
    ONEF = sb("onef", [64], F32)
    EPST = sb("epst", [1], F32)
    AF_ = Arena(nc, "arena_f", 11520, F32)
    AB_ = Arena(nc, "arena_b", 50176, BF16)
    PB = []
    pbig = nc.alloc_psum_tensor("pbig", [P, 4096], F32)
    for i in range(8):
        PB.append(Tile(pbig[:, i * 512:(i + 1) * 512]))
        PB[-1].psum = True
    PB7B = PB[7].ap.bitcast(BF16)

    def dma(eng, out_t, out_ap, in_t, in_ap, partial=False):
        sbt = out_t if out_t is not None else in_t
        if sbt.ds is None:
            sbt.ds = S_.dsem()
        if out_t is not None:
            kw = dict(partial=[out_t]) if partial else dict(writes=[out_t])
        else:
            kw = dict(reads=[in_t])
        S_.op(eng, lambda e: e.dma_start(out=out_ap, in_=in_ap), dma=sbt.ds, **kw)

    def load(out_t, in_ap, out_ap=None, partial=False):
        dma("sp", out_t, out_t.ap if out_ap is None else out_ap, None, in_ap, partial=partial)

    def store(out_ap, in_t, in_ap=None):
        dma("pool", None, out_ap, in_t, in_t.ap if in_ap is None else in_ap)

    def mm(out_t, out_ap, lt, l_ap, rt, r_ap, start, stop):
        S_.op("pe", lambda e: e.matmul(out_ap, l_ap, r_ap, start=start, stop=stop),
              reads=[lt, rt], writes=[out_t])

    def tr(out_t, out_ap, in_t, in_ap, ident_ap, idt):
        S_.op("pe", lambda e: e.transpose(out_ap, in_ap, ident_ap), reads=[in_t, idt], writes=[out_t])

    def act(out_t, out_ap, in_t, in_ap, func, bias=None, scale=None, accum=None, extra=(), partial=False):
        kw = {}
        rd = [in_t] + list(extra)
        wr = [out_t]
        if bias is not None:
            kw["bias"] = bias
        if scale is not None:
            kw["scale"] = scale
        if accum is not None:
            kw["accum_out"] = accum[1]
            wr.append(accum[0])
        if partial:
            S_.op("act", lambda e: e.activation(out=out_ap, in_=in_ap, func=func, **kw), reads=rd, partial=wr)
        else:
            S_.op("act", lambda e: e.activation(out=out_ap, in_=in_ap, func=func, **kw), reads=rd, writes=wr)

    def ts(eng, out_t, out_ap, in_t, in_ap, s1, s2, op0, op1=None, extra=(), partial=False):
        kw = dict(partial=[out_t]) if partial else dict(writes=[out_t])
        if op1 is None:
            S_.op(eng, lambda e: e.tensor_scalar(out=out_ap, in0=in_ap, scalar1=s1, scalar2=None, op0=op0),
                  reads=[in_t] + list(extra), **kw)
        else:
            S_.op(eng, lambda e: e.tensor_scalar(out=out_ap, in0=in_ap, scalar1=s1, scalar2=s2, op0=op0, op1=op1),
                  reads=[in_t] + list(extra), **kw)

    def tt(eng, out_t, out_ap, a_t, a_ap, b_t, b_ap, op, partial=False):
        kw = dict(partial=[out_t]) if partial else dict(writes=[out_t])
        S_.op(eng, lambda e: e.tensor_tensor(out=out_ap, in0=a_ap, in1=b_ap, op=op), reads=[a_t, b_t], **kw)

    def stt(eng, out_t, out_ap, a_t, a_ap, scal, b_t, b_ap, op0, op1, extra=(), partial=False):
        kw = dict(partial=[out_t]) if partial else dict(writes=[out_t])
        S_.op(eng, lambda e: e.scalar_tensor_tensor(out=out_ap, in0=a_ap, scalar=scal, in1=b_ap, op0=op0, op1=op1),
              reads=[a_t, b_t] + list(extra), **kw)

    def cp(eng, out_t, out_ap, in_t, in_ap, partial=False):
        kw = dict(partial=[out_t]) if partial else dict(writes=[out_t])
        if eng == "act":
            S_.op("act", lambda e: e.activation(out=out_ap, in_=in_ap, func=AF.Copy), reads=[in_t], **kw)
        else:
            S_.op(eng, lambda e: e.tensor_copy(out=out_ap, in_=in_ap), reads=[in_t], **kw)

    def red(out_t, out_ap, in_t, in_ap, partial=False):
        kw = dict(partial=[out_t]) if partial else dict(writes=[out_t])
        S_.op("dve", lambda e: e.tensor_reduce(out=out_ap, in_=in_ap, axis=AX.X, op=ALU.add), reads=[in_t], **kw)

    def mset(eng, t, ap, val, partial=False):
        kw = dict(partial=[t]) if partial else dict(writes=[t])
        S_.op(eng, lambda e: e.memset(ap, val), **kw)

    def rstd_from_ss(ss_t, out_t, n):
        rsq(out_t, out_t.ap, ss_t, ss_t.ap, 1.0 / n)

    def recip(t, ap, partial=False):
        kw = dict(partial=[t]) if partial else dict(writes=[t])
        S_.op("dve", lambda e: e.reciprocal(out=ap, in_=ap), reads=[t], **kw)

    def rsq(out_t, out_ap, in_t, in_ap, mult, partial=False):
        np_ = out_ap.shape[0]
        act(out_t, out_ap, in_t, in_ap, AF.Sqrt, bias=EPST.ap[0:np_, :], scale=mult, extra=[EPST], partial=partial)
        recip(out_t, out_ap, partial=partial)

    for t, k in ((IDB, "c_idb"), (IDF, "c_idf"), (MASK, "c_mask"), (QDF, "c_qdf"), (QDB, "c_qdb"), (CDF, "c_cdf"),
                 (CDB, "c_cdb"), (KDF, "c_kdf"), (KDB, "c_kdb"), (K64, "c_k64"), (SELB, "c_selb")):
        flat = t.ap if len(t.ap.shape) == 2 else t.ap.rearrange("p a b -> p (a b)")
        load(t, CIN[k][:, :], out_ap=flat)
    load(SEL, CIN["c_sel"][:, :], out_ap=SEL.ap[0:8, :])
    load(FNW, W["final_norm_w"].partition_broadcast(P))
    mset("dve", ZT, ZT.ap, 0.0)
    mset("dve", ONEF, ONEF.ap, 1.0)
    mset("dve", EPST, EPST.ap, EPS)
    store(ZROW[:, :], ZT, ZT.ap[0:1, :])
    for l in range(depth):
        for off, nm, n in ((B_GN, "ret_gn_w", 384), (B_DWB, "conv_dw_b", 256), (B_LNW, "conv_ln_w", 256),
                           (B_LNB, "conv_ln_b", 256)):
            load(BROWS, W[nm][l].partition_broadcast(P), out_ap=BROWS.ap[:, l, off:off + n], partial=True)

    AF_.reset()
    ctmp = [AF_.get(128) for _ in range(2)]
    cnt = [0]

    def load_cols(src2d, n, l, off):
        t = ctmp[cnt[0] % 2]
        cnt[0] += 1
        load(t, src2d, out_ap=t.ap[0:n, :])
        tr(PB[0], PB[0].ap[:, 0:n], t, t.ap[0:n, :], IDF.ap[0:n, 0:n], IDF)
        cp("dve", COLS, COLS.ap[:, l, off:off + n], PB[0], PB[0].ap[:, 0:n], partial=True)

    for l in range(depth):
        load_cols(W["attn_norm_w"][l].rearrange("(c p) -> c p", p=P), 8, l, C_AN)
        load_cols(W["mla_q_norm_w"][l].rearrange("(c p) -> c p", p=P), 3, l, C_QN)
        load_cols(W["mla_kv_norm_w"][l].rearrange("(c p) -> c p", p=P), 1, l, C_KVN)
        load_cols(W["ffn_norm_w"][l].rearrange("(c p) -> c p", p=P), 8, l, C_FN)
        for k in range(3):
            load_cols(W["ffn_conv_w"][l][k].rearrange("(c p) -> c p", p=P), 44, l, C_FCW + 44 * k)
        load_cols(W["ffn_conv_b"][l].rearrange("(c p) -> c p", p=P), 44, l, C_FCB)
        load_cols(W["conv_pw_b"][l].rearrange("(c p) -> c p", p=P), 2, l, C_PWB)
        for c in range(2):
            load_cols(W["conv_dw_w"][l][:, c * 128:(c + 1) * 128], 31, l, C_DW + 31 * c)
    S_.barrier()

    S_.new_phase()
    AF_.reset()
    AB_.reset()
    pl = [AF_.get(2592) for _ in range(2)]
    po = [AB_.get(2592) for _ in range(2)]
    pc = [0]

    def prep_chunk(src_ap, w, dst_ap, scale_ap, dst_view=None):
        i = pc[0]
        pc[0] += 1
        a, b = pl[i % 2], po[i % 2]
        load(a, src_ap, out_ap=a.ap[:, 0:w])
        oap = b.ap[:, 0:w]
        if scale_ap is None:
            if i % 2 == 0:
                cp("dve", b, oap, a, a.ap[:, 0:w])
            else:
                cp("act", b, oap, a, a.ap[:, 0:w])
        else:
            if i % 2 == 0:
                ts("dve", b, oap, a, a.ap[:, 0:w], scale_ap, None, ALU.mult, extra=[COLS])
            else:
                act(b, oap, a, a.ap[:, 0:w], AF.Copy, scale=scale_ap, extra=[COLS])
        store(dst_ap, b, oap if dst_view is None else dst_view(oap))

    for l in range(depth):
        ws = WS[l]
        for kc in range(8):
            prep_chunk(W["w_in"][l][kc * P:(kc + 1) * P, :], INC, ws["WIN"][:, kc, :], COLS.ap[:, l, C_AN + kc:C_AN + kc + 1])
        for kc in range(3):
            prep_chunk(W["mla_w_uq"][l][kc * P:(kc + 1) * P, :], 576, ws["WUQ"][:, kc, :], COLS.ap[:, l, C_QN + kc:C_QN + kc + 1])
        prep_chunk(W["mla_w_ukv"][l][:, :], 768, ws["WUKV"][:, :], COLS.ap[:, l, C_KVN:C_KVN + 1])
        for kc in range(2):
            prep_chunk(W["conv_pw_w"][l][kc * P:(kc + 1) * P, :], 256, ws["WPW"][:, kc, :], None)
        for kc in range(8):
            prep_chunk(W["w_out"][l][kc * P:(kc + 1) * P, :], D, ws["WOUT"][:, kc, :], None)
        wupv = ws["WUP"].rearrange("j p k c -> p j k c")
        for kc in range(8):
            for half in range(2):
                for sg in range(2):
                    c0 = half * DFF + sg * 1408
                    prep_chunk(W["w_up"][l][kc * P:(kc + 1) * P, c0:c0 + 1408], 1408,
                               wupv[:, sg * 11:(sg + 1) * 11, kc, half * 128:(half + 1) * 128],
                               COLS.ap[:, l, C_FN + kc:C_FN + kc + 1],
                               dst_view=lambda a: a.rearrange("p (j c) -> p j c", c=128))
        for j in range(NJ):
            prep_chunk(W["w_down"][l][j * P:(j + 1) * P, :], D, ws["WDN"][:, j, :], None)
    S_.barrier()
    if stop_after == "prep":
        S_.emit()
        return nc

    def phase_A(l, sn, S, xsrc):
        sc = SC_[sn]
        ws = WS[l]
        NT = S // P
        S_.new_phase()
        AF_.reset()
        AB_.reset()
        win = AB_.get(8, INC)
        wuq = AB_.get(3, 576)
        wukv = AB_.get(768)
        load(win, ws["WIN"][:, :, :])
        load(wuq, ws["WUQ"][:, :, :])
        load(wukv, ws["WUKV"][:, :])
        zpad = AB_.get(2, 15)
        mset("dve", zpad, zpad.ap, 0.0)
        ctv = sc["CT"].rearrange("c p s -> p c s")
        store(ctv[:, :, 0:15], zpad)
        store(ctv[:, :, S + 15:S + 30], zpad)
        kmaxs = [AF_.get(H) for _ in range(2)]

        def stream(sid):
            base = sid * 1536
            MB = Tile(pbig[:, base:base + 1536])
            MB.psum = True
            TBt = PB[6 + sid]
            TBB = TBt.ap.bitcast(BF16)

            def view(c0, w):
                t = Tile(pbig[:, base + c0:base + c0 + w])
                t.buf = MB.buf
                t.psum = True
                return t

            Pcq, Pckv, Pcv = view(0, 512), view(512, 512), view(1024, 512)
            M0, M1, M2 = view(0, 512), view(512, 512), view(1024, 512)
            Pq, Pk, Pv, Pg = view(0, 384), view(384, 384), view(768, 384), view(1152, 384)
            Uq = [view(0, 512), view(512, 512)]
            Ukv = [view(1024, 512), TBt]
            xt = [AF_.get(D) for _ in range(2)]
            rt = [AF_.get(192) for _ in range(2)]
            ssx = AF_.get(1)
            rsx = AF_.get(1)
            hb = AB_.get(D)
            hT = AB_.get(8, 128)
            tA = AF_.get(RW)
            tB = AF_.get(RW)
            qrot = AB_.get(RW)
            krot = AB_.get(RW)
            vb = AB_.get(RW)
            gs = AF_.get(RW)
            ss2 = AF_.get(2)
            rs2 = AF_.get(2)
            cqn = AB_.get(512)
            junk = AB_.get(D)
            tA2 = AF_.get(32)
            tB2 = AF_.get(32)
            krf = AF_.get(32)
            krss = AF_.get(1)
            sig = AF_.get(256)
            cb = AB_.get(256)
            s1 = AB_.get(8, 128)
            cqT = AB_.get(4, 128)
            qaug = AB_.get(H, 128)
            kaug = AB_.get(H, 128)
            va = AB_.get(H, 65)
            qtm = AB_.get(H, 128)
            ktm = AB_.get(H, 128)
            sq = AF_.get(3, 96)
            ssq = AF_.get(H)
            kss = AF_.get(H)
            kmax = kmaxs[sid]
            tqa = AF_.get(3, 32)
            tqb = AF_.get(3, 32)
            mset("dve", qaug, qaug.ap, 0.0)
            mset("dve", kaug, kaug.ap, 0.0)
            mset("dve", kaug, kaug.ap[:, :, 96:97], 1.0)
            mset("dve", va, va.ap, 1.0)
            mset("dve", kmax, kmax.ap, 0.0)

            def rope_ret(Pt, out_t, rtile, pre):
                P3 = Pt.ap[:, 0:384].rearrange("p (h d) -> p h d", d=64)
                cc = bc_mid(rtile.ap[:, 0:64], H)
                ns = bc_mid(rtile.ap[:, 64:96], H)
                sn_ = bc_mid(rtile.ap[:, 96:128], H)
                A3 = tA.ap.rearrange("p (h d) -> p h d", d=64)
                B3 = tB.ap.rearrange("p (h d) -> p h d", d=64)
                stt("dve", tA, A3, Pt, P3, pre, rtile, cc, ALU.mult, ALU.mult)
                stt("dve", tB, B3[:, :, 0:32], Pt, P3[:, :, 32:64], pre, rtile, ns, ALU.mult, ALU.mult)
                stt("dve", tB, B3[:, :, 32:64], Pt, P3[:, :, 0:32], pre, rtile, sn_, ALU.mult, ALU.mult, partial=True)
                tt("dve", out_t, out_t.ap, tA, tA.ap, tB, tB.ap, ALU.add)

            tiles = list(range(sid, NT, 2))

            def ldx(n):
                i_ = tiles[n]
                load(xt[n % 2], xsrc[i_ * P:(i_ + 1) * P, :])
                load(rt[n % 2], CIN["c_rope"][i_ * P:(i_ + 1) * P, :])

            if tiles:
                ldx(0)
            for n, i in enumerate(tiles):
                if n + 1 < len(tiles):
                    ldx(n + 1)
                x, r = xt[n % 2], rt[n % 2]
                t0 = i * P
                act(junk, junk.ap, x, x.ap, AF.Square, accum=(ssx, ssx.ap))
                rstd_from_ss(ssx, rsx, D)
                act(hb, hb.ap, x, x.ap, AF.Copy, scale=rsx.ap, extra=[rsx])
                yield
                for kc in range(8):
                    tr(TBt, TBB[:, kc * 128:(kc + 1) * 128], hb, hb.ap[:, kc * 128:(kc + 1) * 128], IDB.ap, IDB)
                cp("act", hT, hT.ap.rearrange("p a b -> p (a b)"), TBt, TBB[:, 0:1024])
                yield
                for kc in range(8):
                    for (pt, c0, w) in ((Pcq, 1536, 384), (Pckv, 1920, 160), (Pcv, 2080, 512)):
                        mm(pt, pt.ap[:, 0:w], hT, hT.ap[:, kc, :], win, win.ap[:, kc, c0:c0 + w], kc == 0, kc == 7)
                yield
                act(junk, junk.ap[:, 0:384], Pcq, Pcq.ap[:, 0:384], AF.Square, accum=(ss2, ss2.ap[:, 0:1]))
                act(junk, junk.ap[:, 0:128], Pckv, Pckv.ap[:, 0:128], AF.Square, accum=(ss2, ss2.ap[:, 1:2]))
                act(rs2, rs2.ap[:, 0:1], ss2, ss2.ap[:, 0:1], AF.Sqrt, bias=EPST.ap, scale=1.0 / 384, extra=[EPST])
                act(rs2, rs2.ap[:, 1:2], ss2, ss2.ap[:, 1:2], AF.Sqrt, bias=EPST.ap, scale=1.0 / 128, extra=[EPST], partial=True)
                recip(rs2, rs2.ap)
                yield
                act(cqn, cqn.ap[:, 0:384], Pcq, Pcq.ap[:, 0:384], AF.Copy, scale=rs2.ap[:, 0:1], extra=[rs2])
                act(cqn, cqn.ap[:, 384:512], Pckv, Pckv.ap[:, 0:128], AF.Copy, scale=rs2.ap[:, 1:2], extra=[rs2], partial=True)
                yield
                kx = Pckv.ap[:, 128:160]
                tt("dve", tA2, tA2.ap, Pckv, kx, r, r.ap[:, 128:160], ALU.mult)
                tt("dve", tB2, tB2.ap[:, 0:16], Pckv, kx[:, 16:32], r, r.ap[:, 160:176], ALU.mult)
                tt("dve", tB2, tB2.ap[:, 16:32], Pckv, kx[:, 0:16], r, r.ap[:, 176:192], ALU.mult, partial=True)
                tt("dve", krf, krf.ap, tA2, tA2.ap, tB2, tB2.ap, ALU.add)
                yield
                act(junk, junk.ap[:, 0:32], krf, krf.ap, AF.Square, accum=(krss, krss.ap))
                cp("dve", kaug, kaug.ap[:, :, 64:96], krf, bc_mid(krf.ap, H), partial=True)
                act(sig, sig.ap, Pcv, Pcv.ap[:, 256:512], AF.Tanh, scale=0.5)
                yield
                ts("dve", sig, sig.ap, sig, sig.ap, 0.5, 0.5, ALU.mult, ALU.add)
                tt("dve", cb, cb.ap, Pcv, Pcv.ap[:, 0:256], sig, sig.ap, ALU.mult)
                yield
                for kc in range(8):
                    for (pt, c0) in ((M0, 0), (M1, 512), (M2, 1024)):
                        mm(pt, pt.ap, hT, hT.ap[:, kc, :], win, win.ap[:, kc, c0:c0 + 512], kc == 0, kc == 7)
                yield
                rope_ret(Pq, qrot, r, 1.0)
                yield
                rope_ret(Pk, krot, r, 0.125)
                yield
                cp("act", vb, vb.ap, Pv, Pv.ap[:, 0:384])
                act(gs, gs.ap, Pg, Pg.ap[:, 0:384], AF.Silu)
                yield
                for b_ in range(3):
                    tr(TBt, TBB[:, b_ * 128:(b_ + 1) * 128], qrot, qrot.ap[:, b_ * 128:(b_ + 1) * 128], IDB.ap, IDB)
                for b_ in range(3):
                    tr(TBt, TBB[:, (3 + b_) * 128:(4 + b_) * 128], krot, krot.ap[:, b_ * 128:(b_ + 1) * 128], IDB.ap, IDB)
                for b_ in range(2):
                    tr(TBt, TBB[:, (6 + b_) * 128:(7 + b_) * 128], cb, cb.ap[:, b_ * 128:(b_ + 1) * 128], IDB.ap, IDB)
                cp("act", s1, s1.ap.rearrange("p a b -> p (a b)"), TBt, TBB[:, 0:1024])
                yield
                for b_ in range(4):
                    tr(TBt, TBB[:, b_ * 128:(b_ + 1) * 128], cqn, cqn.ap[:, b_ * 128:(b_ + 1) * 128], IDB.ap, IDB)
                cp("dve", cqT, cqT.ap.rearrange("p a b -> p (a b)"), TBt, TBB[:, 0:512])
                yield
                for hh in range(2):
                    for kc in range(3):
                        mm(Uq[hh], Uq[hh].ap[:, 0:288], cqT, cqT.ap[:, kc, :], wuq, wuq.ap[:, kc, hh * 288:(hh + 1) * 288],
                           kc == 0, kc == 2)
                    mm(Ukv[hh], Ukv[hh].ap[:, 0:384], cqT, cqT.ap[:, 3, :], wukv, wukv.ap[:, hh * 384:(hh + 1) * 384],
                       True, True)
                yield
                for hh in range(2):
                    P3 = Uq[hh].ap[:, 0:288].rearrange("p (h d) -> p h d", d=96)
                    hs = slice(3 * hh, 3 * hh + 3)
                    act(qaug, qaug.ap[:, hs, 0:64], Uq[hh], P3[:, :, 0:64], AF.Copy, scale=SC, partial=True)
                    act(sq, sq.ap, Uq[hh], P3, AF.Square)
                    yield
                    stt("dve", tqa, tqa.ap, Uq[hh], P3[:, :, 64:96], SC, r, bc_mid(r.ap[:, 128:160], 3), ALU.mult, ALU.mult)
                    stt("dve", tqb, tqb.ap[:, :, 0:16], Uq[hh], P3[:, :, 80:96], SC, r, bc_mid(r.ap[:, 160:176], 3), ALU.mult, ALU.mult)
                    stt("dve", tqb, tqb.ap[:, :, 16:32], Uq[hh], P3[:, :, 64:80], SC, r, bc_mid(r.ap[:, 176:192], 3), ALU.mult,
                        ALU.mult, partial=True)
                    yield
                    tt("dve", qaug, qaug.ap[:, hs, 64:96], tqa, tqa.ap, tqb, tqb.ap, ALU.add, partial=True)
                    red(ssq, ssq.ap[:, hs], sq, sq.ap, partial=(hh == 1))
                    yield
                act(ssq, ssq.ap, ssq, ssq.ap, AF.Sqrt)
                yield
                ts("dve", qaug, qaug.ap[:, :, 96:97], ssq, ssq.ap.rearrange("p (h o) -> p h o", o=1), -SC, None, ALU.mult,
                   partial=True)
                for hh in range(2):
                    P3 = Ukv[hh].ap[:, 0:384].rearrange("p (h d) -> p h d", d=128)
                    hs = slice(3 * hh, 3 * hh + 3)
                    act(kaug, kaug.ap[:, hs, 0:64], Ukv[hh], P3[:, :, 0:64], AF.Copy, partial=True)
                    act(sq, sq.ap[:, :, 0:64], Ukv[hh], P3[:, :, 0:64], AF.Square)
                    yield
                    cp("dve", va, va.ap[:, hs, 0:64], Ukv[hh], P3[:, :, 64:128], partial=True)
                    red(kss, kss.ap[:, hs], sq, sq.ap[:, :, 0:64], partial=(hh == 1))
                    yield
                ts("dve", kss, kss.ap, kss, kss.ap, krss.ap, None, ALU.add, extra=[krss])
                tt("dve", kmax, kmax.ap, kmax, kmax.ap, kss, kss.ap, ALU.max)
                yield
                for hq in range(H):
                    tr(TBt, TBB[:, hq * 128:(hq + 1) * 128], qaug, qaug.ap[:, hq, :], IDB.ap, IDB)
                cp("act", qtm, qtm.ap.rearrange("p a b -> p (a b)"), TBt, TBB[:, 0:768])
                yield
                for hq in range(H):
                    tr(TBt, TBB[:, hq * 128:(hq + 1) * 128], kaug, kaug.ap[:, hq, :], IDB.ap, IDB)
                cp("dve", ktm, ktm.ap.rearrange("p a b -> p (a b)"), TBt, TBB[:, 0:768])
                yield
                store(sc["QTR"].rearrange("b p s -> p b s")[:, :, t0:t0 + P], s1, s1.ap[:, 0:3, :])
                store(sc["KTR"].rearrange("b p s -> p b s")[:, :, t0:t0 + P], s1, s1.ap[:, 3:6, :])
                store(ctv[:, :, 15 + t0:15 + t0 + P], s1, s1.ap[:, 6:8, :])
                store(sc["KR"][t0:t0 + P, :], krot)
                store(sc["VR"][t0:t0 + P, :], vb)
                store(sc["G"][t0:t0 + P, :], gs)
                store(sc["QTM"].rearrange("h p s -> p h s")[:, :, t0:t0 + P], qtm)
                store(sc["KTM"].rearrange("h p s -> p h s")[:, :, t0:t0 + P], ktm)
                store(sc["VM"].rearrange("h p n c -> p h n c")[:, :, i, :], va)
                yield

        gens = [stream(0), stream(1)]
        while gens:
            for g_ in list(gens):
                try:
                    next(g_)
                except StopIteration:
                    gens.remove(g_)
        S_.barrier()
        kmax = kmaxs[0]
        tt("dve", kmax, kmax.ap, kmax, kmax.ap, kmaxs[1], kmaxs[1].ap, ALU.max)
        tr(PB[0], PB[0].ap[0:H, 0:128], kmax, kmax.ap, IDF.ap, IDF)
        km1 = AF_.get(1)
        S_.op("dve", lambda e: e.tensor_reduce(out=km1.ap[0:H, :], in_=PB[0].ap[0:H, 0:128], axis=AX.X, op=ALU.max),
              reads=[PB[0]], writes=[km1])
        act(km1, km1.ap[0:H, :], km1, km1.ap[0:H, :], AF.Sqrt)
        dg = AF_.get(H)
        ts("dve", dg, dg.ap[0:H, :], IDF, IDF.ap[0:H, 0:H], km1.ap[0:H, 0:1], None, ALU.mult, extra=[km1])
        mm(PB[1], PB[1].ap[:, 0:H], SEL, SEL.ap[0:H, :], dg, dg.ap[0:H, :], True, True)
        tt("dve", KMX[sn], KMX[sn].ap, PB[1], PB[1].ap[:, 0:H], K64, K64.ap, ALU.add)
        S_.barrier()

    KMX = {sn: sb("kmx_" + sn, [H], F32) for (sn, _) in seqs}

    def phase_B1(l, sn, S):
        sc = SC_[sn]
        N = S // P
        sball = AB_.get(N, 3, 64)
        kt_ = [AB_.get(RW) for _ in range(2)]
        vt_ = [AB_.get(RW) for _ in range(2)]
        qT = [AB_.get(3, 128) for _ in range(2)]
        kT = [AB_.get(3, 128) for _ in range(2)]
        gt_ = [AF_.get(RW) for _ in range(2)]
        kd = AB_.get(RW)
        sm = AB_.get(H, 128)
        qf = AB_.get(3, 2, 128)
        qb = AB_.get(3, 2, 128)
        qz = [AB_.get(3, 2, 128) for _ in range(2)]
        for t_ in (qf, qb, qz[0], qz[1]):
            mset("dve", t_, t_.ap, 0.0)
        sf = AF_.get(3, 64)
        sbk = AF_.get(3, 64)
        sfb = [AB_.get(3, 64) for _ in range(2)]
        y1 = AF_.get(H, 64)
        y2 = AF_.get(H, 64)
        st = AF_.get(4, H)
        yo = AB_.get(RW)
        yts = [AB_.get(3, 128) for _ in range(2)]
        gnw = BROWS.ap[:, l, B_GN:B_GN + 384]

        def state_update(Pt, k_t, v_t, kdec, state, cdec, out_t, out_ap):
            tt("pool", kd, kd.ap, k_t, k_t.ap, kdec, kdec.ap, ALU.mult)
            for b_ in range(3):
                mm(Pt, Pt.ap[:, b_ * 128:(b_ + 1) * 128], kd, kd.ap[:, b_ * 128:(b_ + 1) * 128], v_t,
                   v_t.ap[:, b_ * 128:(b_ + 1) * 128], True, True)
            P3 = Pt.ap[:, 0:384].rearrange("p (b c) -> p b c", c=128)
            tt("dve", state, state.ap, state, state.ap, cdec, cdec.ap, ALU.mult)
            tt("dve", state, state.ap[0:64], state, state.ap[0:64], Pt, P3[0:64, :, 0:64], ALU.add, partial=True)
            tt("dve", state, state.ap[64:128], state, state.ap[64:128], Pt, P3[64:128, :, 64:128], ALU.add, partial=True)
            cp("act", out_t, out_ap, state, state.ap, partial=True)

        mset("dve", sbk, sbk.ap, 0.0)
        mset("dve", sf, sf.ap, 0.0)

        def ld1(c):
            load(kt_[c % 2], sc["KR"][c * P:(c + 1) * P, :])
            load(vt_[c % 2], sc["VR"][c * P:(c + 1) * P, :])

        if N > 1:
            ld1(N - 1)
        for c in range(N - 1, 0, -1):
            yield
            if c - 1 >= 1:
                ld1(c - 1)
            state_update(PB[4], kt_[c % 2], vt_[c % 2], KDB, sbk, CDB, sball, sball.ap[:, c - 1])

        if OPT.get('b1cut', 9) <= 1:
            return
        def ld2(c):
            load(kt_[c % 2], sc["KR"][c * P:(c + 1) * P, :])
            load(vt_[c % 2], sc["VR"][c * P:(c + 1) * P, :])
            load(qT[c % 2], sc["QTR"].rearrange("b p s -> p b s")[:, :, c * P:(c + 1) * P])
            load(kT[c % 2], sc["KTR"].rearrange("b p s -> p b s")[:, :, c * P:(c + 1) * P])
            load(gt_[c % 2], sc["G"][c * P:(c + 1) * P, :])

        ld2(0)
        for c in range(N):
            yield
            if c + 1 < N:
                ld2(c + 1)
            k_t, v_t, q_T, k_T, g_t = kt_[c % 2], vt_[c % 2], qT[c % 2], kT[c % 2], gt_[c % 2]
            q_Z = qz[c % 2]
            cp("dve", q_Z, q_Z.ap[0:64, :, 0, :], q_T, q_T.ap[0:64], partial=True)
            cp("dve", q_Z, q_Z.ap[64:128, :, 1, :], q_T, q_T.ap[64:128], partial=True)
            for hq in range(H):
                b_, pp = hq // 2, hq % 2
                bank = PB[hq // 3]
                o = (hq % 3) * 128
                mm(bank, bank.ap[:, o:o + 128], k_T, k_T.ap[:, b_, :], q_Z, q_Z.ap[:, b_, pp, :], True, True)
            yield
            for hb_ in range(2):
                tt("dve", sm, sm.ap[:, 3 * hb_:3 * hb_ + 3, :].rearrange("p a b -> p (a b)"), PB[hb_], PB[hb_].ap[:, 0:384],
                   MASK, MASK.ap[:, 3 * hb_:3 * hb_ + 3, :].rearrange("p a b -> p (a b)"), ALU.mult, partial=(hb_ == 1))
            if OPT.get('b1cut', 9) <= 2:
                continue
            yield
            if c > 0:
                tt("dve", qf, qf.ap[0:64, :, 0, :], q_T, q_T.ap[0:64], QDF, QDF.ap[0:64], ALU.mult, partial=True)
                tt("dve", qf, qf.ap[64:128, :, 1, :], q_T, q_T.ap[64:128], QDF, QDF.ap[64:128], ALU.mult, partial=True)
            if c < N - 1:
                tt("dve", qb, qb.ap[0:64, :, 0, :], q_T, q_T.ap[0:64], QDB, QDB.ap[0:64], ALU.mult, partial=True)
                tt("dve", qb, qb.ap[64:128, :, 1, :], q_T, q_T.ap[64:128], QDB, QDB.ap[64:128], ALU.mult, partial=True)
            yield
            Py = PB[2]
            sfc = sfb[c % 2]
            for hq in range(H):
                b_, pp = hq // 2, hq % 2
                ps_ = slice(pp * 64, (pp + 1) * 64)
                last_i = not (c > 0 or c < N - 1)
                mm(Py, Py.ap[:, hq * 64:(hq + 1) * 64], sm, sm.ap[:, hq, :], v_t, v_t.ap[:, hq * 64:(hq + 1) * 64], True, last_i)
                if c > 0:
                    mm(Py, Py.ap[:, hq * 64:(hq + 1) * 64], qf, qf.ap[:, b_, pp, :], sfc, sfc.ap[:, b_, :], False, not (c < N - 1))
                if c < N - 1:
                    mm(Py, Py.ap[:, hq * 64:(hq + 1) * 64], qb, qb.ap[:, b_, pp, :], sball, sball.ap[:, c, b_, :], False, True)
            if OPT.get('b1cut', 9) <= 3:
                continue
            yield
            if c < N - 1:
                nx = sfb[(c + 1) % 2]
                state_update(PB[3], k_t, v_t, KDF, sf, CDF, nx, nx.ap)
            if OPT.get('b1cut', 9) <= 4:
                continue
            yield
            Py3 = Py.ap[:, 0:384].rearrange("p (h v) -> p h v", v=64)
            red(st, st.ap[:, 0, :], Py, Py3)
            act(y1, y1.ap, Py, Py3, AF.Square)
            red(st, st.ap[:, 1, :], y1, y1.ap, partial=True)
            yield
            ts("dve", st, st.ap[:, 0:2, :], st, st.ap[:, 0:2, :], 1.0 / 64, None, ALU.mult)
            tt("dve", st, st.ap[:, 2, :], st, st.ap[:, 0, :], st, st.ap[:, 0, :], ALU.mult, partial=True)
            tt("dve", st, st.ap[:, 3, :], st, st.ap[:, 1, :], st, st.ap[:, 2, :], ALU.subtract, partial=True)
            ts("dve", st, st.ap[:, 3, :], st, st.ap[:, 3, :], 0.0, None, ALU.max, partial=True)
            rsq(st, st.ap[:, 3, :], st, st.ap[:, 3, :], 1.0, partial=True)
            if OPT.get('b1cut', 9) <= 5:
                continue
            yield
            tt("dve", y2, y2.ap, Py, Py3, st, bc_last(st.ap[:, 0, :], 64), ALU.subtract)
            tt("pool", y1, y1.ap, y2, y2.ap, st, bc_last(st.ap[:, 3, :], 64), ALU.mult)
            yield
            y1f = y1.ap.rearrange("p h v -> p (h v)")
            y2f = y2.ap.rearrange("p h v -> p (h v)")
            tt("pool", y2, y2f, y1, y1f, BROWS, gnw, ALU.mult)
            tt("dve", yo, yo.ap, y2, y2f, g_t, g_t.ap, ALU.mult)
            if OPT.get('b1cut', 9) <= 6:
                continue
            yield
            for b_ in range(3):
                tr(PB[7], PB7B[:, b_ * 128:(b_ + 1) * 128], yo, yo.ap[:, b_ * 128:(b_ + 1) * 128], IDB.ap, IDB)
            ys = yts[c % 2]
            cp("act", ys, ys.ap.rearrange("p a b -> p (a b)"), PB[7], PB7B[:, 0:384])
            store(sc["YT"].rearrange("k p s -> p k s")[:, 0:3, c * P:(c + 1) * P], ys)

    def phase_B2(l, sn, S):
        sc = SC_[sn]
        NB = S // P
        QC = 512
        NQ = S // QC
        S_.new_phase()
        AF_.reset()
        AB_.reset()
        KT = [AB_.get(S) for _ in range(2)]
        VH = [AB_.get(NB, 65) for _ in range(2)]
        QTc = [AB_.get(QC) for _ in range(2)]
        PT = [AB_.get(QC) for _ in range(4)]
        rden = AF_.get(QC)
        mset("dve", rden, rden.ap, 0.0)
        bcs = AF_.get(QC)
        yo = [AB_.get(QC) for _ in range(2)]
        kmx = KMX[sn]

        def ldh(hq):
            load(KT[hq % 2], sc["KTM"][hq, :, :])
            load(VH[hq % 2], sc["VM"][hq, :, :, :])

        it = [0]

        def ldq(hq, qi):
            load(QTc[it[0] % 2], sc["QTM"][hq, :, qi * QC:(qi + 1) * QC])

        ldh(0)
        ldq(0, 0)
        for hq in range(H):
            if hq + 1 < H:
                ldh(hq + 1)
            K_, V_ = KT[hq % 2], VH[hq % 2]
            ts("dve", K_, K_.ap[64:128, :], K_, K_.ap[64:128, :], kmx.ap[64:128, hq:hq + 1], None, ALU.mult, extra=[kmx])
            for qi in range(NQ):
                cur = it[0]
                it[0] += 1
                nh, nq = (hq, qi + 1) if qi + 1 < NQ else (hq + 1, 0)
                if nh < H:
                    ldq(nh, nq)
                Q_ = QTc[cur % 2]
                Po = PB[4 + cur % 2]

                def pv(n):
                    mm(Po, Po.ap[0:65, :], V_, V_.ap[:, n, :], PT[n % 4], PT[n % 4].ap, n == 0, n == NB - 1)

                for n in range(NB):
                    ps_ = PB[n % 4]
                    mm(ps_, ps_.ap, K_, K_.ap[:, n * P:(n + 1) * P], Q_, Q_.ap, True, True)
                    act(PT[n % 4], PT[n % 4].ap, ps_, ps_.ap, AF.Exp)
                    if n >= 3:
                        pv(n - 3)
                for n in range(max(NB - 3, 0), NB):
                    pv(n)
                S_.op("dve", lambda e, Po=Po: e.reciprocal(out=rden.ap[64:65, :], in_=Po.ap[64:65, :]), reads=[Po], partial=[rden])
                mm(PB[6], PB[6].ap[0:64, :], SELB, SELB.ap, rden, rden.ap, True, True)
                cp("act", bcs, bcs.ap[0:64, :], PB[6], PB[6].ap[0:64, :])
                y_ = yo[cur % 2]
                tt("dve", y_, y_.ap[0:64, :], Po, Po.ap[0:64, :], bcs, bcs.ap[0:64, :], ALU.mult)
                kc = 3 + hq // 2
                pp = hq % 2
                store(sc["YT"][kc, pp * 64:(pp + 1) * 64, qi * QC:(qi + 1) * QC], y_, y_.ap[0:64, :])
        S_.barrier()

    def phase_B3(l, sn, S):
        sc = SC_[sn]
        NT = S // P
        diag = AB_.get(62, 128)
        wpw = AB_.get(2, 256)
        load(wpw, WS[l]["WPW"][:, :, :])
        for c in range(2):
            for j in range(31):
                col = C_DW + 31 * c + j
                ts("dve", diag, diag.ap[:, c * 31 + j, :], IDB, IDB.ap, COLS.ap[:, l, col:col + 1], None,
                   ALU.mult, extra=[COLS], partial=True)
        cw = [AB_.get(2, 158) for _ in range(2)]
        x1 = AF_.get(256)
        x2 = AF_.get(256)
        st = AF_.get(8)
        junk = AF_.get(256)
        cs = AB_.get(256)
        c2T = AB_.get(2, 128)
        ys = [AB_.get(2, 128) for _ in range(2)]
        ctv = sc["CT"].rearrange("c p s -> p c s")
        dwb = BROWS.ap[:, l, B_DWB:B_DWB + 256]
        lnw = BROWS.ap[:, l, B_LNW:B_LNW + 256]
        lnb = BROWS.ap[:, l, B_LNB:B_LNB + 256]
        load(cw[0], ctv[:, :, 0:158])
        for i in range(NT):
            yield
            yield
            if i + 1 < NT:
                load(cw[(i + 1) % 2], ctv[:, :, (i + 1) * P:(i + 1) * P + 158])
            w_ = cw[i % 2]
            Pc = PB[5]
            for c in range(2):
                if c == 1:
                    yield
                for j in range(31):
                    mm(Pc, Pc.ap[:, c * 128:(c + 1) * 128], w_, w_.ap[:, c, j:j + 128], diag, diag.ap[:, c * 31 + j, :],
                       j == 0, j == 30)
            yield
            tt("dve", x1, x1.ap, Pc, Pc.ap[:, 0:256], BROWS, dwb, ALU.add)
            act(junk, junk.ap, x1, x1.ap, AF.Square, accum=(st, st.ap[:, 1:2]))
            red(st, st.ap[:, 0:1], x1, x1.ap, partial=True)
            yield
            ts("dve", st, st.ap[:, 0:2], st, st.ap[:, 0:2], 1.0 / 256, None, ALU.mult)
            tt("dve", st, st.ap[:, 2:3], st, st.ap[:, 0:1], st, st.ap[:, 0:1], ALU.mult, partial=True)
            tt("dve", st, st.ap[:, 3:4], st, st.ap[:, 1:2], st, st.ap[:, 2:3], ALU.subtract, partial=True)
            ts("dve", st, st.ap[:, 3:4], st, st.ap[:, 3:4], 0.0, None, ALU.max, partial=True)
            rsq(st, st.ap[:, 3:4], st, st.ap[:, 3:4], 1.0, partial=True)
            yield
            ts("dve", x2, x2.ap, x1, x1.ap, st.ap[:, 0:1], st.ap[:, 3:4], ALU.subtract, ALU.mult, extra=[st])
            tt("pool", x1, x1.ap, x2, x2.ap, BROWS, lnw, ALU.mult)
            tt("pool", x2, x2.ap, x1, x1.ap, BROWS, lnb, ALU.add)
            yield
            act(cs, cs.ap, x2, x2.ap, AF.Silu)
            for c in range(2):
                tr(PB[7], PB7B[:, c * 128:(c + 1) * 128], cs, cs.ap[:, c * 128:(c + 1) * 128], IDB.ap, IDB)
            cp("dve", c2T, c2T.ap.rearrange("p a b -> p (a b)"), PB[7], PB7B[:, 0:256])
            yield
            Pp = PB[6]
            for oc in range(2):
                for kc in range(2):
                    mm(Pp, Pp.ap[:, oc * 128:(oc + 1) * 128], wpw, wpw.ap[:, kc, oc * 128:(oc + 1) * 128], c2T, c2T.ap[:, kc, :],
                       kc == 0, kc == 1)
            yield
            y_ = ys[i % 2]
            for oc in range(2):
                act(y_, y_.ap[:, oc, :], Pp, Pp.ap[:, oc * 128:(oc + 1) * 128], AF.Identity,
                    bias=COLS.ap[:, l, C_PWB + oc:C_PWB + oc + 1], scale=1.0, extra=[COLS], partial=(oc == 1))
            store(sc["YT"].rearrange("k p s -> p k s")[:, 6:8, i * P:(i + 1) * P], y_)

    def phase_C1(l, sn, S, xsrc):
        sc = SC_[sn]
        NT = S // P
        S_.new_phase()
        AF_.reset()
        AB_.reset()
        wo = AB_.get(8, D)
        load(wo, WS[l]["WOUT"][:, :, :])
        yt = [AB_.get(8, 128) for _ in range(2)]
        xt = [AF_.get(D) for _ in range(2)]
        xo = [AF_.get(D) for _ in range(2)]
        ytv = sc["YT"].rearrange("k p s -> p k s")

        def ld(i):
            load(yt[i % 2], ytv[:, :, i * P:(i + 1) * P])
            load(xt[i % 2], xsrc[i * P:(i + 1) * P, :])

        ld(0)
        for i in range(NT):
            if i + 1 < NT:
                ld(i + 1)
            y_ = yt[i % 2]
            banks = (PB[0], PB[1]) if i % 2 == 0 else (PB[2], PB[3])
            for cg in range(2):
                for kc in range(8):
                    mm(banks[cg], banks[cg].ap, y_, y_.ap[:, kc, :], wo, wo.ap[:, kc, cg * 512:(cg + 1) * 512], kc == 0, kc == 7)
            o_ = xo[i % 2]
            for cg in range(2):
                tt("dve", o_, o_.ap[:, cg * 512:(cg + 1) * 512], banks[cg], banks[cg].ap, xt[i % 2],
                   xt[i % 2].ap[:, cg * 512:(cg + 1) * 512], ALU.add, partial=(cg == 1))
            store(sc["X1"][i * P:(i + 1) * P, :], o_)
        S_.barrier()

    def phase_C2(l, sn, S, last):
        sc = SC_[sn]
        ws = WS[l]
        TT = 512
        NTT = S // TT
        S_.new_phase()
        AF_.reset()
        AB_.reset()
        wd = AB_.get(NJ, D)
        load(wd, ws["WDN"][:, :, :])
        wu = [AB_.get(8, 256) for _ in range(2)]
        x1 = [AF_.get(D) for _ in range(4)]
        xh = AF_.get(D)
        ssx = AF_.get(1)
        rsx = AF_.get(1)
        junk = AB_.get(D)
        hb = AB_.get(D)
        h2T = AB_.get(8, 514)
        accg2 = [AF_.get(512) for _ in range(2)]
        accu2 = [AF_.get(512) for _ in range(2)]
        sg2 = [AF_.get(512) for _ in range(2)]
        actT = AB_.get(NJ, 512)
        xo = [AF_.get(D) for _ in range(2)]
        x1s = sc["X1"]
        wi = [0]

        def ldw(j):
            load(wu[wi[0] % 2], ws["WUP"][j, :, :, :])
            wi[0] += 1

        def conv_branch(Pt, Ph, hoff, ch, acc):
            w0 = COLS.ap[:, l, C_FCW + ch:C_FCW + ch + 1]
            w1 = COLS.ap[:, l, C_FCW + 44 + ch:C_FCW + 44 + ch + 1]
            w2 = COLS.ap[:, l, C_FCW + 88 + ch:C_FCW + 88 + ch + 1]
            bb = COLS.ap[:, l, C_FCB + ch:C_FCB + ch + 1]
            act(acc, acc.ap, Pt, Pt.ap, AF.Identity, bias=bb, scale=w1, extra=[COLS])
            stt("dve", acc, acc.ap[:, 1:512], Pt, Pt.ap[:, 0:511], w0, acc, acc.ap[:, 1:512], ALU.mult, ALU.add, extra=[COLS])
            stt("dve", acc, acc.ap[:, 0:511], Pt, Pt.ap[:, 1:512], w2, acc, acc.ap[:, 0:511], ALU.mult, ALU.add, extra=[COLS])
            stt("dve", acc, acc.ap[:, 0:1], Ph, Ph.ap[:, hoff:hoff + 1], w0, acc, acc.ap[:, 0:1], ALU.mult, ALU.add, extra=[COLS])
            stt("dve", acc, acc.ap[:, 511:512], Ph, Ph.ap[:, hoff + 1:hoff + 2], w2, acc, acc.ap[:, 511:512], ALU.mult, ALU.add,
                extra=[COLS])

        ldw(0)
        for ti in range(NTT):
            t0 = ti * TT
            for s in range(4):
                load(x1[s], x1s[t0 + s * P:t0 + (s + 1) * P, :])
            load(xh, (x1s[t0 - 1:t0, :] if t0 > 0 else ZROW[:, :]), out_ap=xh.ap[0:1, :])
            load(xh, (x1s[t0 + TT:t0 + TT + 1, :] if t0 + TT < S else ZROW[:, :]), out_ap=xh.ap[1:2, :], partial=True)
            for s in range(5):
                xs = x1[s] if s < 4 else xh
                nr = P if s < 4 else 2
                act(junk, junk.ap[0:nr, :], xs, xs.ap[0:nr, :], AF.Square, accum=(ssx, ssx.ap[0:nr, :]))
                rsq(rsx, rsx.ap[0:nr, :], ssx, ssx.ap[0:nr, :], 1.0 / D)
                act(hb, hb.ap[0:nr, :], xs, xs.ap[0:nr, :], AF.Copy, scale=rsx.ap[0:nr, 0:1], extra=[rsx])
                for kc in range(8):
                    tr(PB[7], PB7B[:, kc * 128:kc * 128 + nr], hb, hb.ap[0:nr, kc * 128:(kc + 1) * 128], IDB.ap[0:nr, 0:nr], IDB)
                src = PB7B[:, 0:1024].rearrange("p (k t) -> p k t", t=128)[:, :, 0:nr]
                cp("act", h2T, h2T.ap[:, :, s * P:s * P + nr], PB[7], src, partial=(s > 0))
            for j in range(NJ):
                if not (ti == NTT - 1 and j == NJ - 1):
                    ldw((j + 1) % NJ)
                w_ = wu[(ti * NJ + j) % 2]
                Pg_, Pu_ = PB[j % 2], PB[2 + j % 2]
                Ph = PB[4] if j % 2 == 0 else PB[7]
                ho = 0
                accg, accu, sg = accg2[j % 2], accu2[j % 2], sg2[j % 2]
                for (Pt, hoff, c0) in ((Pg_, ho, 0), (Pu_, ho + 2, 128)):
                    for kc in range(8):
                        mm(Pt, Pt.ap, w_, w_.ap[:, kc, c0:c0 + 128], h2T, h2T.ap[:, kc, 0:512], kc == 0, kc == 7)
                    for kc in range(8):
                        mm(Ph, Ph.ap[:, hoff:hoff + 2], w_, w_.ap[:, kc, c0:c0 + 128], h2T, h2T.ap[:, kc, 512:514], kc == 0, kc == 7)
                conv_branch(Pg_, Ph, ho, j, accg)
                conv_branch(Pu_, Ph, ho + 2, NJ + j, accu)
                act(sg, sg.ap, accg, accg.ap, AF.Silu)
                tt("pool!", actT, actT.ap[:, j, :], sg, sg.ap, accu, accu.ap, ALU.mult, partial=(j > 0))
            for s in range(4):
                o_ = xo[s % 2]
                for cg in range(2):
                    Pd = PB[5 + cg]
                    for j in range(NJ):
                        mm(Pd, Pd.ap, actT, actT.ap[:, j, s * P:(s + 1) * P], wd, wd.ap[:, j, cg * 512:(cg + 1) * 512],
                           j == 0, j == NJ - 1)
                    tt("dve", o_, o_.ap[:, cg * 512:(cg + 1) * 512], Pd, Pd.ap, x1[s], x1[s].ap[:, cg * 512:(cg + 1) * 512],
                       ALU.add, partial=(cg == 1))
                r0 = t0 + s * P
                if not last:
                    store(sc["X2"][r0:r0 + P, :], o_)
                else:
                    act(junk, junk.ap, o_, o_.ap, AF.Square, accum=(ssx, ssx.ap))
                    rstd_from_ss(ssx, rsx, D)
                    stt("dve", o_, o_.ap, o_, o_.ap, rsx.ap, FNW, FNW.ap, ALU.mult, ALU.mult, extra=[rsx])
                    store(YOUT[sn][r0:r0 + P, :], o_)
        S_.barrier()

    done = False
    for l in range(depth):
        for (sn, S) in seqs:
            xsrc = XIN[sn] if l == 0 else SC_[sn]["X2"]
            for ph in ("A", "B13", "B2", "C1", "C2"):
                if ph == "A":
                    phase_A(l, sn, S, xsrc)
                elif ph == "B13":
                    S_.new_phase()
                    AF_.reset()
                    AB_.reset()
                    gens = [phase_B1(l, sn, S), phase_B3(l, sn, S)]
                    while gens:
                        for g_ in list(gens):
                            try:
                                next(g_)
                            except StopIteration:
                                gens.remove(g_)
                    S_.barrier()
                elif ph == "B2":
                    phase_B2(l, sn, S)
                elif ph == "C1":
                    phase_C1(l, sn, S, xsrc)
                else:
                    phase_C2(l, sn, S, l == depth - 1)
                if stop_after == (l, sn, ph):
                    done = True
                    break
            if done:
                break
        if done:
            break
    S_.emit()
    return nc


_CACHE = {}
OPT = {'attach': True}


def kernel(**inputs):
    SP_, SS_ = 8192, 2048
    depth = 2
    ncores = 8
    key = "main"
    if key not in _CACHE:
        _CACHE[key] = build([("p", SP_), ("s", SS_)], depth, SP_)
    nc = _CACHE[key]
    hc = host_consts(SP_)
    xp = np.asarray(inputs["x_prompt"], dtype=np.float32)
    xs = np.asarray(inputs["x_sample"], dtype=np.float32)
    base = {n: np.ascontiguousarray(np.asarray(inputs[n], dtype=np.float32)) for n in WNAMES}
    base.update(hc)
    in_maps = []
    for c in range(ncores):
        m = dict(base)
        m["x_p"] = np.ascontiguousarray(xp[c])
        m["x_s"] = np.ascontiguousarray(xs[c])
        in_maps.append(m)
    res = run_bass_kernel_spmd(nc, in_maps, core_ids=list(range(ncores)))
    yp = np.stack([np.asarray(res.results[c]["y_p"], dtype=np.float32) for c in range(ncores)], axis=0)
    ys = np.stack([np.asarray(res.results[c]["y_s"], dtype=np.float32) for c in range(ncores)], axis=0)
    return (yp, ys)
```
